# Optimizing a Trainium2 kernel written in Bass

```python
import math
import jax, jax.numpy as jnp
from jax import lax
import numpy as np

D_MODEL = 1024
BATCH = 1
SEQ = 16384
DEPTH = 1

CHUNK = 64
D_A = D_MODEL
D_B = D_MODEL
GROUP = 64
CONV_A_WIDTH = 31
CONV_B_WIDTH = 3
D_FF = 4 * D_MODEL
N_ADA = 6
D_IN = 2 * D_A + 3 * D_B + 2 * D_MODEL
LN_EPS = 1e-5
DEEPNORM_ALPHA = (2.0 * DEPTH) ** 0.25
DEEPNORM_BETA = (8.0 * DEPTH) ** -0.25

kernel_name = "hybrid_conformer_shortconv_gated_block"


def _layernorm(x, g=None, b=None):
    xf = x.astype(jnp.float32)
    mu = jnp.mean(xf, axis=-1, keepdims=True)
    var = jnp.mean(jnp.square(xf - mu), axis=-1, keepdims=True)
    y = (xf - mu) * lax.rsqrt(var + LN_EPS)
    if g is not None:
        y = y * g.astype(jnp.float32) + b.astype(jnp.float32)
    return y.astype(x.dtype)


def _causal_depthwise_conv(u, w):
    k = w.shape[0]
    u_pad = jnp.pad(u, ((0, 0), (k - 1, 0), (0, 0)))
    return lax.conv_general_dilated(
        u_pad, w[:, None, :].astype(u.dtype), window_strides=(1,), padding="VALID",
        dimension_numbers=("NWC", "WIO", "NWC"), feature_group_count=u.shape[-1])


def setup_inputs(seed: int = 0) -> dict:
    key = jax.random.key(seed)
    ks = jax.random.split(key, 24)
    f32 = jnp.float32
    L, D = DEPTH, D_MODEL

    def nrm(k, shape, scale):
        return jax.random.normal(k, shape, f32) * scale

    return {
        "x": nrm(ks[0], (BATCH, SEQ, D), 1.0),
        "c": nrm(ks[1], (BATCH, D), 1.0),
        "w_ada": nrm(ks[2], (L, D, N_ADA * D), 0.2 * D ** -0.5),
        "b_ada": nrm(ks[3], (L, N_ADA * D), 0.02),
        "w_in": nrm(ks[4], (L, D, D_IN), D ** -0.5),
        "b_in": nrm(ks[5], (L, D_IN), 0.02),
        "conv_a_w": nrm(ks[6], (L, CONV_A_WIDTH, D_A), CONV_A_WIDTH ** -0.5),
        "conv_a_b": nrm(ks[7], (L, D_A), 0.02),
        "ln_a_g": 1.0 + nrm(ks[8], (L, D_A), 0.02),
        "ln_a_b": nrm(ks[9], (L, D_A), 0.02),
        "w_a_out": nrm(ks[10], (L, D_A, D), D_A ** -0.5),
        "b_a_out": nrm(ks[11], (L, D), 0.02),
        "conv_b_w": nrm(ks[12], (L, CONV_B_WIDTH, D_B), CONV_B_WIDTH ** -0.5),
        "w_b_out": nrm(ks[13], (L, D_B, D), D_B ** -0.5),
        "w_o": nrm(ks[14], (L, D, D), DEEPNORM_BETA * D ** -0.5),
        "b_o": nrm(ks[15], (L, D), 0.02),
        "ln1_g": 1.0 + nrm(ks[16], (L, D), 0.02),
        "ln1_b": nrm(ks[17], (L, D), 0.02),
        "w_up": nrm(ks[18], (L, D, D_FF), D ** -0.5),
        "b_up": nrm(ks[19], (L, D_FF), 0.02),
        "w_down": nrm(ks[20], (L, D_FF, D), DEEPNORM_BETA * D_FF ** -0.5),
        "b_down": nrm(ks[21], (L, D), 0.02),
        "ln2_g": 1.0 + nrm(ks[22], (L, D), 0.02),
        "ln2_b": nrm(ks[23], (L, D), 0.02),
    }


def reference(x, c, w_ada, b_ada, w_in, b_in, conv_a_w, conv_a_b, ln_a_g, ln_a_b,
              w_a_out, b_a_out, conv_b_w, w_b_out, w_o, b_o, ln1_g, ln1_b,
              w_up, b_up, w_down, b_down, ln2_g, ln2_b):
    split_idx = np.cumsum([D_A, D_A, D_B, D_B, D_B, D_MODEL]).tolist()
    c_act = jax.nn.silu(c)
    for l in range(DEPTH):
        mod = c_act @ w_ada[l] + b_ada[l]
        shift1, scale1, gate1, shift2, scale2, gate2 = [
            m[:, None, :] for m in jnp.split(mod, N_ADA, axis=-1)]

        h = _layernorm(x) * (1.0 + scale1) + shift1
        z = jnp.einsum("bsd,de->bse", h, w_in[l]) + b_in[l]
        a_val, a_gate, b_gb, b_gc, b_x, g_a, g_b = jnp.split(z, split_idx, axis=-1)

        u = a_val * jax.nn.sigmoid(a_gate)
        u = _causal_depthwise_conv(u, conv_a_w[l]) + conv_a_b[l]
        u = jax.nn.silu(_layernorm(u, ln_a_g[l], ln_a_b[l]))
        y_a = jnp.einsum("bsc,cd->bsd", u, w_a_out[l]) + b_a_out[l]

        v = b_gb * _causal_depthwise_conv(b_gc * b_x, conv_b_w[l])
        y_b = jnp.einsum("bsc,cd->bsd", v, w_b_out[l])

        merged = jax.nn.sigmoid(g_a) * y_a + jax.nn.sigmoid(g_b) * y_b
        out = jnp.einsum("bsd,de->bse", merged, w_o[l]) + b_o[l]
        x = _layernorm(DEEPNORM_ALPHA * x + (1.0 + gate1) * out, ln1_g[l], ln1_b[l])

        h = _layernorm(x) * (1.0 + scale2) + shift2
        f = jnp.square(jax.nn.relu(jnp.einsum("bsd,df->bsf", h, w_up[l]) + b_up[l]))
        out = jnp.einsum("bsf,fd->bsd", f, w_down[l]) + b_down[l]
        x = _layernorm(DEEPNORM_ALPHA * x + (1.0 + gate2) * out, ln2_g[l], ln2_b[l])
    return x
```

```python
import bisect
import numpy as np
import concourse.bass as bass
import concourse.mybir as mybir
from concourse.bass_utils import run_bass_kernel_spmd

F32 = mybir.dt.float32
BF16 = mybir.dt.bfloat16
AF = mybir.ActivationFunctionType
ALU = mybir.AluOpType

NCORES = 8
D = 1024
KC = 8
SEQ = 16384
TPC = SEQ // NCORES
NB = 2
TB = TPC // NB
NS = TB // 512
NT = TB // 128
HALO = 32
TW = HALO + TB
DFF = 4096
FC = DFF // 128
ALPHA = float(2.0 ** 0.25)
EPS = 1e-5

PV_BIN, PV_CAB, PV_LAG, PV_LAB, PV_BAO, PV_BUP, PV_CAW, PV_CBW, PV_C, PV_MASK = 0, 56, 64, 72, 80, 88, 120, 368, 392, 400
NPV = 404


class Res:
    __slots__ = ("name", "w", "r", "dsem", "dcnt", "lo", "hi", "dead", "excl")

    def __init__(self, name, lo=None, hi=None):
        self.name = name
        self.w = None
        self.r = {}
        self.dsem = None
        self.dcnt = 0
        self.lo = lo
        self.hi = hi
        self.dead = False
        self.excl = False


class Eng:
    def __init__(self, fw, handle, name, selfsync=True):
        self.fw = fw
        self.h = handle
        self.name = name
        self.sem = fw.nc.alloc_semaphore("s_" + name)
        self.idx = 0
        self.cnt = 0
        self.seen = {}
        self.selfsync = selfsync

    def wait_ev(self, ev, same_ok=False):
        if ev is None:
            return
        sem, val, key = ev
        if key == self.name and (same_ok or not self.selfsync):
            return
        sid = id(sem)
        if self.seen.get(sid, 0) >= val:
            return
        self.h.wait_ge(sem, self.fw.sem_value(key, val))
        self.seen[sid] = val

    def deps(self, reads, writes):
        for r in reads:
            assert not r.dead, ("read of dead res", r.name)
            self.wait_ev(r.w)
            if r.excl:
                for ev in r.r.values():
                    self.wait_ev(ev, same_ok=True)
        for w in writes:
            assert not w.dead, ("write of dead res", w.name)
            self.wait_ev(w.w)
            for ev in w.r.values():
                self.wait_ev(ev)

    def mark(self, inst, reads=(), writes=()):
        self.idx += 1
        if self.fw.needs_inc(self.name, self.idx):
            self.cnt += 1
            inst.then_inc(self.sem, 1)
        ev = (self.sem, self.idx, self.name)
        for r in reads:
            r.r[self.name] = ev
        for w in writes:
            w.w = ev
            w.r = {}
        return ev

    def op(self, fn, reads=(), writes=()):
        self.deps(reads, writes)
        inst = fn(self.h)
        self.mark(inst, reads, writes)
        return inst

    def dma(self, out, in_, owner, reads=(), writes=()):
        self.deps(reads, writes)
        if owner.dsem is None:
            self.fw.nsem += 1
            owner.dsem = self.fw.nc.alloc_semaphore("d%d_%s" % (self.fw.nsem, owner.name))
        owner.dcnt += 16
        inst = self.h.dma_start(out=out, in_=in_)
        inst.then_inc(owner.dsem, 16)
        ev = (owner.dsem, owner.dcnt, "dma_" + owner.name)
        for r in reads:
            r.r["dma_%s_%d" % (owner.name, owner.dcnt)] = ev
        for w in writes:
            w.w = ev
            w.r = {}
        return ev


class PEGroup:
    def __init__(self, pe, bank):
        self.pe = pe
        self.bank = bank
        self.first = True
        self.reads = {}

    def mm(self, out, lhsT, rhs, reads, last=False, mark=()):
        pe = self.pe
        if self.first:
            pe.deps(reads, [self.bank])
        else:
            pe.deps(reads, [])
        inst = pe.h.matmul(out, lhsT, rhs, start=self.first, stop=last)
        self.first = False
        for r in reads:
            self.reads[id(r)] = r
        if last:
            pe.mark(inst, list(self.reads.values()), [self.bank])
            self.reads = {}
        elif mark:
            pe.mark(inst, list(mark), [])
            for r in mark:
                self.reads.pop(id(r), None)
        return inst


class FW:
    def __init__(self, nc, needed=None):
        self.nc = nc
        self.needed = needed
        self.rec = {}
        self.registry = []
        self.nsem = 0
        self.pe = Eng(self, nc.tensor, "pe", selfsync=False)
        self.act = Eng(self, nc.scalar, "act")
        self.dve = Eng(self, nc.vector, "dve")
        self.pool = Eng(self, nc.gpsimd, "pool")
        self.sp = Eng(self, nc.sync, "sp")

    def needs_inc(self, key, idx):
        if self.needed is None:
            return True
        lst = self.needed.get(key, [])
        i = bisect.bisect_left(lst, idx)
        return i < len(lst) and lst[i] == idx

    def sem_value(self, key, val):
        if key.startswith("dma_"):
            return val
        if self.needed is None:
            self.rec.setdefault(key, set()).add(val)
            return val
        lst = self.needed[key]
        i = bisect.bisect_left(lst, val)
        assert i < len(lst) and lst[i] == val, (key, val)
        return i + 1

    def new_res(self, name, lo, hi):
        r = Res(name, lo, hi)
        keep = []
        for o in self.registry:
            if o.lo < hi and lo < o.hi:
                if o.w is not None:
                    k = o.w[2] if not o.w[2].startswith("dma_") else o.w[2] + str(o.w[1])
                    if k not in r.r or r.r[k][1] < o.w[1]:
                        r.r[k] = o.w
                for k, ev in o.r.items():
                    if k not in r.r or r.r[k][1] < ev[1]:
                        r.r[k] = ev
                o.dead = True
                for glo, ghi in ((o.lo, lo), (hi, o.hi)):
                    if glo < ghi:
                        gh = Res(o.name + "~", glo, ghi)
                        gh.w = o.w
                        gh.r = dict(o.r)
                        gh.dead = True
                        keep.append(gh)
            else:
                keep.append(o)
        keep.append(r)
        self.registry = keep
        return r


def new_group(fw, names, lo, hi):
    base = fw.new_res("grp", lo, hi)
    fw.registry.remove(base)
    out = []
    for n in names:
        r = Res(n, lo, hi)
        r.r = dict(base.r)
        out.append(r)
    fw.registry.extend(out)
    return out


class Rot:
    def __init__(self, items):
        self.items = list(items)
        self.i = 0

    def next(self):
        x = self.items[self.i % len(self.items)]
        self.i += 1
        return x


def build_program(needed=None):
    nc = bass.Bass("TRN2", target_bir_lowering=False)
    fw = FW(nc, needed)
    pe, act, dve, pool, sp = fw.pe, fw.act, fw.dve, fw.pool, fw.sp

    def din(name, shape):
        return nc.dram_tensor(name, shape, F32, kind="ExternalInput").ap()

    xs = din("xs", [TPC + HALO, D])
    pvec_d = din("pvec", [128, NPV])
    rows_d = din("rows", [1, 8192])
    bc_d = din("bc", [128, 4096])
    w_ada = din("w_ada", [D, 6 * D])
    w_in = din("w_in", [D, 7 * D])
    w_a_out = din("w_a_out", [D, D])
    w_b_out = din("w_b_out", [D, D])
    w_o = din("w_o", [D, D])
    w_up = din("w_up", [D, DFF])
    w_down = din("w_down", [DFF, D])
    y = nc.dram_tensor("y", [TPC, D], F32, kind="ExternalOutput").ap()

    C_SZ = 38400
    W0 = C_SZ
    NWB = 4
    A0 = W0 + NWB * 8192
    A_SZ = 132096 + 4096
    TOTAL = A0 + A_SZ
    lo, hi = nc.bump_sbuf(TOTAL)
    cnt = [0]

    def sb(off, shape, dtype, name):
        cnt[0] += 1
        return nc.alloc_sbuf_tensor_at("%s_%d" % (name, cnt[0]), list(shape), dtype, offset=lo + off).ap()

    def newres(name, off, nbytes):
        return fw.new_res(name, off, off + nbytes)

    o = 0
    pvec = sb(o, [128, NPV], F32, "pvec"); o += 1664
    eps_t = sb(o, [128, 1], F32, "eps"); eps2_t = sb(o + 32, [128, 1], F32, "eps2"); o += 64
    ident = sb(o, [128, 128], BF16, "ident"); o += 256
    ones_b = sb(o, [128, 128], BF16, "onesb"); o += 256
    ones_f = sb(o, [1, 128], F32, "onesf"); o += 512
    dg3 = sb(o, [128, 24, 128], BF16, "dg3"); o += 6144
    borow = sb(o, [128, D], BF16, "borow"); o += 2048
    bdrow = sb(o, [128, D], BF16, "bdrow"); o += 2048
    g1b = sb(o, [128, D], F32, "g1b"); o += 4096
    g2b = sb(o, [128, D], F32, "g2b"); o += 4096
    bc = sb(o, [128, 4096], F32, "bc"); o += 16384
    modT = sb(o, [128, 32], F32, "modT"); o += 128
    cact = sb(o, [128, 8], BF16, "cact"); o += 64
    NST = 4
    st_sb, mv_sb, sd_sb, nm_sb, st_res = [], [], [], [], []
    for i in range(NST):
        st_sb.append(sb(o, [128, 2, 6], F32, "st")); o += 64
        mv_sb.append(sb(o, [128, 2], F32, "mv")); o += 32
        sd_sb.append(sb(o, [128, 1], F32, "sd")); o += 32
        nm_sb.append(sb(o, [128, 1], F32, "nm")); o += 32
        st_res.append(Res("st%d" % i))
    assert o <= C_SZ, o
    st_rot = Rot(range(NST))
    r_pvec, r_const, r_bc, r_rows, r_mod, r_dg3 = Res("pvec"), Res("const"), Res("bc"), Res("rows"), Res("mod"), Res("dg3")
    r_cact, r_ident = Res("cact"), Res("ident")

    wb_ap = [sb(W0 + i * 8192, [128, 8, 512], BF16, "wb") for i in range(NWB)]
    wb_res = [Res("wb%d" % i) for i in range(NWB)]

    bank_ap = [nc.alloc_psum_tensor("bank%d" % i, [128, 512], F32).ap() for i in range(8)]
    bank_res = [Res("bank%d" % i) for i in range(8)]
    for r_ in bank_res:
        r_.excl = True

    OFF_HT = A0
    OFF_XT = [A0 + 16896 + i * 4096 for i in range(3)]
    OFF_XN = [A0 + 29184 + i * 2048 for i in range(2)]
    OFF_TMP = A0 + 16896
    OFF_X1 = A0 + 33280
    OFF_FT = A0 + 66048
    OFF_U = A0 + 33280
    OFF_CB = A0 + 50176
    OFF_U2 = A0 + 82944
    OFF_V = A0 + 99328
    OFF_MG = A0 + 115712
    OFF_TQ = A0 + 132096

    def wsrc(w, r0, c0):
        return w.rearrange("(kc p) c -> p kc c", p=128)[:, r0 // 128:r0 // 128 + 8, c0:c0 + 512]

    wtiles = []
    wscale = {}
    for nt in range(4):
        wtiles.append(wsrc(w_ada, 0, nt * 512))
    for b in range(NB):
        for g in range(2):
            wtiles.append(wsrc(w_in, 0, 1024 + g * 512))
            wtiles.append(wsrc(w_in, 0, g * 512))
        if b == 0:
            for nt in range(4, 12):
                wtiles.append(wsrc(w_ada, 0, nt * 512))
        for g in range(2):
            wtiles.append(wsrc(w_in, 0, 4096 + g * 512))
            wtiles.append(wsrc(w_in, 0, 3072 + g * 512))
        for g in range(2):
            wtiles.append(wsrc(w_in, 0, 2048 + g * 512))
        for g in range(2):
            wtiles.append(wsrc(w_in, 0, 5120 + g * 512))
            wtiles.append(wsrc(w_a_out, 0, g * 512))
        for g in range(2):
            wtiles.append(wsrc(w_in, 0, 6144 + g * 512))
            wtiles.append(wsrc(w_b_out, 0, g * 512))
        for h in range(2):
            wscale[len(wtiles)] = ("g1", h)
            wtiles.append(wsrc(w_o, 0, h * 512))
        for g in range(8):
            wtiles.append(wsrc(w_up, 0, g * 512))
        for h in range(2):
            for gq in range(4):
                wscale[len(wtiles)] = ("g2", h)
                wtiles.append(wsrc(w_down, gq * 1024, h * 512))

    ws = {"issued": 0, "released": -1, "next": 0, "pending": None}

    def ws_prescale(i):
        which, h = wscale[i]
        gb = g1b if which == "g1" else g2b
        wt, wr = wb_ap[i % NWB], wb_res[i % NWB]
        pool.op(lambda e: e.tensor_tensor(wt, wt, gb[:, h * 512:(h + 1) * 512].unsqueeze(1).broadcast_to([128, 8, 512]), ALU.mult),
                reads=[wr, r_mod2], writes=[wr])

    def ws_flush():
        if ws["pending"] is not None:
            ws_prescale(ws["pending"])
            ws["pending"] = None

    def ws_pump():
        while ws["issued"] < len(wtiles) and ws["issued"] <= ws["released"] + NWB:
            i = ws["issued"]
            s = i % NWB
            pool.dma(wb_ap[s], wtiles[i], wb_res[s], writes=[wb_res[s]])
            ws["issued"] += 1
            ws_flush()
            if i in wscale:
                ws["pending"] = i

    def ws_get():
        i = ws["next"]
        ws["next"] += 1
        assert i < ws["issued"], "weight tile not yet issued"
        if ws["pending"] == i:
            ws_flush()
        return wb_ap[i % NWB], wb_res[i % NWB], i

    def ws_release(i):
        assert i == ws["released"] + 1, (i, ws["released"])
        ws["released"] = i
        ws_pump()

    sp.dma(pvec, pvec_d, r_pvec, writes=[r_pvec])
    ws_pump()
    mod_row = sb(OFF_V, [1, 6144], F32, "modrow")
    identf = sb(OFF_U2, [128, 128], F32, "identf")
    r_modrow = newres("modrow", OFF_V, 24576)
    r_identf = newres("identf", OFF_U2, 512)
    sp.dma(mod_row, rows_d[0:1, 0:6144], r_modrow, writes=[r_modrow])
    sp.dma(bc, bc_d, r_bc, writes=[r_bc])
    rowstage = sb(OFF_V + 24576, [1, 2048], F32, "rowstage")
    r_rowstage = newres("rowstage", OFF_V + 24576, 8192)
    sp.dma(rowstage, rows_d[0:1, 6144:8192], r_rowstage, writes=[r_rowstage])
    dve.op(lambda e: e.memset(eps_t, EPS), writes=[r_const])
    dve.op(lambda e: e.memset(eps2_t, EPS / (ALPHA * ALPHA)), writes=[r_const])
    dve.op(lambda e: e.memset(ones_b, 1.0), writes=[r_const])
    dve.op(lambda e: e.memset(ones_f, 1.0), writes=[r_const])
    pool.op(lambda e: e.memset(identf, 1.0), writes=[r_identf])
    pool.op(lambda e: e.affine_select(out=identf, in_=identf, pattern=[[-1, 128]], compare_op=ALU.is_equal,
                                      fill=0.0, base=0, channel_multiplier=1), reads=[r_identf], writes=[r_identf])
    dve.op(lambda e: e.tensor_copy(ident, identf), reads=[r_identf], writes=[r_ident])
    e0 = sb(OFF_TQ + 3584, [128, 128], BF16, "e0")
    dve.op(lambda e: e.memset(e0, 0.0), writes=[r_const])
    dve.op(lambda e: e.memset(e0[0:1, :], 1.0), writes=[r_const])
    dve.op(lambda e: e.memset(borow, 0.0), writes=[r_rows])
    dve.op(lambda e: e.memset(bdrow, 0.0), writes=[r_rows])
    zrow = sb(OFF_TQ + 2560, [1, 512], BF16, "zrow")
    dve.op(lambda e: e.memset(zrow, 0.0), writes=[r_const])
    ident32 = sb(OFF_TQ + 2048, [128, 128], F32, "ident32")
    r_ident32 = Res("ident32")
    dve.op(lambda e: e.tensor_copy(ident32, identf), reads=[r_identf], writes=[r_ident32])
    pool.op(lambda e: e.tensor_tensor(dg3, ident.unsqueeze(1).broadcast_to([128, 24, 128]),
                                      pvec[:, PV_CBW:PV_CBW + 24].unsqueeze(2).broadcast_to([128, 24, 128]), ALU.mult),
            reads=[r_ident, r_pvec], writes=[r_dg3])
    act.op(lambda e: e.activation(cact, pvec[:, PV_C:PV_C + 8], AF.Silu), reads=[r_pvec], writes=[r_cact])

    brot = Rot(range(6))
    SH1, SC1, SH2, SC2 = 0, 8, 16, 24
    r_mod1, r_mod2 = Res("mod1"), Res("mod2")

    def ada_tile(nt):
        wt, wr, wi = ws_get()
        bi = brot.next()
        g = PEGroup(pe, bank_res[bi])
        for kc in range(KC):
            g.mm(bank_ap[bi][0:1, :], cact[:, kc:kc + 1], wt[:, kc, :], [r_cact, wr], last=(kc == KC - 1))
        ws_release(wi)
        dve.op(lambda e: e.tensor_tensor(mod_row[0:1, nt * 512:(nt + 1) * 512], bank_ap[bi][0:1, :],
                                         mod_row[0:1, nt * 512:(nt + 1) * 512], ALU.add),
               reads=[bank_res[bi], r_modrow], writes=[r_modrow])

    def mod_cols(pairs, rres):
        bi = brot.next()
        for cb, sec, _ in pairs:
            for j in range(8):
                g = PEGroup(pe, bank_res[bi])
                g.mm(bank_ap[bi][:, cb + j:cb + j + 1], mod_row[0:1, sec * 1024 + j * 128: sec * 1024 + (j + 1) * 128],
                     ones_f[0:1, 0:1], [r_modrow, r_const])
                g.mm(bank_ap[bi][:, cb + j:cb + j + 1], ones_b[0:1, :], zrow[0:1, 0:1], [r_const], last=True)
        for cb, sec, one in pairs:
            if one:
                dve.op(lambda e: e.tensor_scalar(modT[:, cb:cb + 8], bank_ap[bi][:, cb:cb + 8], 1.0, None, ALU.add),
                       reads=[bank_res[bi]], writes=[rres])
            else:
                dve.op(lambda e: e.tensor_copy(modT[:, cb:cb + 8], bank_ap[bi][:, cb:cb + 8]), reads=[bank_res[bi]], writes=[rres])

    def mod_gates():
        for gb, sec in ((g1b, 2), (g2b, 5)):
            for h in range(2):
                bi = brot.next()
                g = PEGroup(pe, bank_res[bi])
                g.mm(bank_ap[bi], ones_f[0:1, 0:128], mod_row[0:1, sec * 1024 + h * 512: sec * 1024 + (h + 1) * 512],
                     [r_modrow, r_const])
                g.mm(bank_ap[bi], ones_b[0:1, :], zrow[0:1, :], [r_const], last=True)
                dve.op(lambda e: e.tensor_scalar(gb[:, h * 512:(h + 1) * 512], bank_ap[bi], 1.0, 1.0 / ALPHA, ALU.add, ALU.mult),
                       reads=[bank_res[bi]], writes=[r_mod2])

    def mkslots(n, base_off=None):
        out = []
        for i in range(n):
            off = base_off + i * 160
            out.append(dict(st=sb(off, [128, 2, 6], F32, "st"), mv=sb(off + 64, [128, 2], F32, "mv"),
                            sd=sb(off + 96, [128, 1], F32, "sd"), nm=sb(off + 128, [128, 1], F32, "nm"),
                            res=newres("stslot", off, 160)))
        return out

    cslots = [dict(st=st_sb[i], mv=mv_sb[i], sd=sd_sb[i], nm=nm_sb[i], res=st_res[i]) for i in range(NST)]

    def ln_stats_a(src, npart, sl, rsrc, halves=(0, 1), aggr=True):
        st, mv, res = sl["st"], sl["mv"], sl["res"]
        for hh in halves:
            dve.op(lambda e: e.bn_stats(st[:npart, hh, :], src[:npart, hh * 512:(hh + 1) * 512]), reads=[rsrc, res], writes=[res])
        if aggr:
            dve.op(lambda e: e.bn_aggr(mv[:npart, :], st[:npart, :, :]), reads=[res], writes=[res])

    def ln_sqrt(npart, sl, eps_ap):
        act.op(lambda e: e.activation(sl["sd"][:npart, :], sl["mv"][:npart, 1:2], AF.Sqrt, bias=eps_ap[:npart, :], scale=1.0),
               reads=[sl["res"], r_const], writes=[sl["res"]])

    def ln_recip(npart, sl, nmr=True):
        dve.op(lambda e: e.reciprocal(sl["sd"][:npart, :], sl["sd"][:npart, :]), reads=[sl["res"]], writes=[sl["res"]])
        if nmr:
            dve.op(lambda e: e.tensor_scalar(sl["nm"][:npart, :], sl["mv"][:npart, 0:1], sl["sd"][:npart, 0:1], -1.0, ALU.mult, ALU.mult),
                   reads=[sl["res"]], writes=[sl["res"]])

    def ln_affine_dve(dst, npart, sl, rdst, g_b, b_b):
        dve.op(lambda e: e.scalar_tensor_tensor(dst, dst, sl["mv"][:npart, 0:1], g_b, ALU.subtract, ALU.mult),
               reads=[rdst, sl["res"], r_bc], writes=[rdst])
        dve.op(lambda e: e.scalar_tensor_tensor(dst, dst, sl["sd"][:npart, 0:1], b_b, ALU.mult, ALU.add),
               reads=[rdst, sl["res"], r_bc], writes=[rdst])

    pst_rot = Rot((6, 7))

    def transpose_evac(src_bf, npart, rsrc, dst, rdst, col0, sc_col, sh_col, rmod, n_dve=0, bank=None):
        if bank is not None:
            ba = bb = bank
            n_dve = 0
        elif n_dve == 0:
            ba = bb = pst_rot.next()
        else:
            ba, bb = 6, 7
        psa = bank_ap[ba].bitcast(BF16).rearrange("p (k t) -> p k t", k=8)
        psb = bank_ap[bb].bitcast(BF16).rearrange("p (k t) -> p k t", k=8)
        wr_banks = [bank_res[ba]] if ba == bb else [bank_res[ba], bank_res[bb]]
        pe.deps([rsrc, r_ident], wr_banks)
        inst = None
        for kc in range(KC):
            ps = psb if kc >= KC - n_dve else psa
            inst = pe.h.transpose(ps[:, kc, 0:npart], src_bf[:npart, kc * 128:(kc + 1) * 128], ident[:npart, :npart])
        pe.mark(inst, [rsrc, r_ident], wr_banks)
        for kc in range(KC):
            if sc_col is None:
                if kc >= KC - n_dve:
                    dve.op(lambda e: e.tensor_copy(dst[:, kc, col0:col0 + npart], psb[:, kc, 0:npart]),
                           reads=[bank_res[bb]], writes=[rdst[1]])
                else:
                    act.op(lambda e: e.activation(dst[:, kc, col0:col0 + npart], psa[:, kc, 0:npart], AF.Identity),
                           reads=[bank_res[ba]], writes=[rdst[0]])
            elif kc >= KC - n_dve:
                dve.op(lambda e: e.tensor_scalar(dst[:, kc, col0:col0 + npart], psb[:, kc, 0:npart],
                                                 modT[:, sc_col + kc:sc_col + kc + 1], modT[:, sh_col + kc:sh_col + kc + 1],
                                                 ALU.mult, ALU.add),
                       reads=[bank_res[bb], rmod], writes=[rdst[1]])
            else:
                act.op(lambda e: e.activation(dst[:, kc, col0:col0 + npart], psa[:, kc, 0:npart], AF.Identity,
                                              bias=modT[:, sh_col + kc:sh_col + kc + 1], scale=modT[:, sc_col + kc:sc_col + kc + 1]),
                       reads=[bank_res[ba], rmod], writes=[rdst[0]])

    out_events = []

    def p1_setup(b, low_banks=False, raw=False):
        row0 = b * TB
        hT = sb(OFF_HT, [128, KC, TW], BF16, "hT")
        _g = new_group(fw, ["hT%d%s" % (t, ab) for t in range(NT + 1) for ab in "ab"], OFF_HT, OFF_HT + 16896)
        r_hT = [(_g[2 * t], _g[2 * t + 1]) for t in range(NT + 1)]
        xt = [sb(OFF_XT[i], [128, D], F32, "xt") for i in range(3)]
        r_xt = [newres("xt%d" % i, OFF_XT[i], 4096) for i in range(3)]
        xn = [sb(OFF_XN[i], [128, D], BF16, "xn") for i in range(2)]
        r_xn = [newres("xn%d" % i, OFF_XN[i], 2048) for i in range(2)]
        tiles = [("h", HALO, row0, 0)] + [(t, 128, row0 + HALO + t * 128, HALO + t * 128) for t in range(NT)]
        nti = len(tiles)
        sis = {}

        def p1_m0(i):
            _, npart, r0, _ = tiles[i]
            sp.dma(xt[i % 3][:npart, :], xs[r0:r0 + npart, :], r_xt[i % 3], writes=[r_xt[i % 3]])

        def p1_a(i):
            _, npart, _, _ = tiles[i]
            sis[i] = cslots[st_rot.next()]
            ln_stats_a(xt[i % 3], npart, sis[i], r_xt[i % 3])

        def p1_b(i):
            _, npart, _, _ = tiles[i]
            ln_sqrt(npart, sis[i], eps_t)

        def p1_c(i):
            _, npart, _, _ = tiles[i]
            sl = sis[i]
            if low_banks:
                ln_recip(npart, sl, nmr=True)
                act.op(lambda e: e.activation(xn[i % 2][:npart, :], xt[i % 3][:npart, :], AF.Identity,
                                              bias=sl["nm"][:npart, :], scale=sl["sd"][:npart, 0:1]),
                       reads=[r_xt[i % 3], sl["res"]], writes=[r_xn[i % 2]])
            else:
                ln_recip(npart, sl, nmr=False)
                dve.op(lambda e: e.tensor_scalar(xn[i % 2][:npart, :], xt[i % 3][:npart, :], sl["mv"][:npart, 0:1], sl["sd"][:npart, 0:1],
                                                 ALU.subtract, ALU.mult),
                       reads=[r_xt[i % 3], sl["res"]], writes=[r_xn[i % 2]])

        def p1_m2(i):
            _, npart, _, c0 = tiles[i]
            if low_banks:
                transpose_evac(xn[i % 2], npart, r_xn[i % 2], hT, r_hT[i], c0, SC1, SH1, r_mod1, bank=i % 2)
            elif raw:
                transpose_evac(xn[i % 2], npart, r_xn[i % 2], hT, r_hT[i], c0, None, None, None, n_dve=3)
            else:
                transpose_evac(xn[i % 2], npart, r_xn[i % 2], hT, r_hT[i], c0, SC1, SH1, r_mod1, n_dve=3)

        stages = [p1_m0, lambda i: (p1_a(i), p1_b(i)), p1_c, p1_m2]
        steps = []
        for step in range(nti + len(stages) - 1):
            def run(step=step):
                for sidx in reversed(range(len(stages))):
                    i = step - sidx
                    if 0 <= i < nti:
                        stages[sidx](i)
            steps.append(run)
        return dict(hT=hT, r_hT=r_hT, tiles=tiles), steps

    ctx, steps = p1_setup(0, raw=True)
    nada = 0
    for k_, st_ in enumerate(steps):
        st_()
        if k_ >= 3 and nada < 4:
            ada_tile(nada)
            nada += 1
    while nada < 4:
        ada_tile(nada)
        nada += 1
    mod_cols([(SH1, 0, False), (SC1, 1, True)], r_mod1)
    for i_, (_, npart_, _, c0_) in enumerate(ctx["tiles"]):
        ra_, rb_ = ctx["r_hT"][i_]
        for kc in range(KC):
            view = ctx["hT"][:, kc, c0_:c0_ + npart_]
            if kc < 5:
                dve.op(lambda e: e.tensor_scalar(view, view, modT[:, SC1 + kc:SC1 + kc + 1], modT[:, SH1 + kc:SH1 + kc + 1],
                                                 ALU.mult, ALU.add),
                       reads=[ra_, rb_, r_mod1], writes=[ra_])
            else:
                act.op(lambda e: e.activation(view, view, AF.Identity, bias=modT[:, SH1 + kc:SH1 + kc + 1],
                                              scale=modT[:, SC1 + kc:SC1 + kc + 1]),
                       reads=[ra_, rb_, r_mod1], writes=[rb_])

    nsteps = []
    for b in range(NB):
        row0 = b * TB
        hT, r_hT = ctx["hT"], ctx["r_hT"]

        def hT_reads(seg):
            tl_ = [r_hT[0]] if seg == 0 else r_hT[1 + (seg - 1) * 4: 1 + seg * 4]
            return [r for pr in tl_ for r in pr]

        segs = [(0, HALO), (HALO, 512), (HALO + 512, 512)]

        u = sb(OFF_U, [128, KC, TW], BF16, "u")
        r_u = [[newres("u%d_%d" % (j, s), OFF_U + (j * TW + segs[s][0]) * 2, segs[s][1] * 2) for s in range(3)] for j in range(8)]
        sg = [sb(OFF_TMP + i * 2048, [128, 512], F32, "sg") for i in range(2)]
        r_sg = [newres("sg%d" % i, OFF_TMP + i * 2048, 2048) for i in range(2)]
        sg_rot = Rot(range(2))
        brot = Rot(range(6))
        mask_col = pvec[:, PV_MASK + b:PV_MASK + b + 1]
        dg = [sb(OFF_U2 + i * 7936, [128, 31, 128], BF16, "dg31") for i in range(2)]
        r_dg = [newres("dg31_%d" % i, OFF_U2 + i * 7936, 7936) for i in range(2)]

        def build_dg(j_):
            dve.op(lambda e: e.tensor_tensor(dg[j_ % 2], ident.unsqueeze(1).broadcast_to([128, 31, 128]),
                                             pvec[:, PV_CAW + j_ * 31:PV_CAW + (j_ + 1) * 31].unsqueeze(2).broadcast_to([128, 31, 128]),
                                             ALU.mult),
                   reads=[r_ident, r_pvec], writes=[r_dg[j_ % 2]])

        gw = [ws_get() for _ in range(4)]

        def glu(j, s):
            g, jj = divmod(j, 4)
            (wg, rg, _), (wv, rv, _) = gw[2 * g], gw[2 * g + 1]
            c0, n = segs[s]
            bi = brot.next()
            grp = PEGroup(pe, bank_res[bi])
            for kc in range(KC):
                grp.mm(bank_ap[bi][:, 0:n], wg[:, kc, jj * 128:(jj + 1) * 128], hT[:, kc, c0:c0 + n],
                       [rg] + hT_reads(s), last=(kc == KC - 1))
            k = sg_rot.next()
            act.op(lambda e: e.activation(sg[k][:, 0:n], bank_ap[bi][:, 0:n], AF.Sigmoid,
                                          bias=pvec[:, PV_BIN + 8 + j:PV_BIN + 9 + j], scale=1.0),
                   reads=[bank_res[bi], r_pvec], writes=[r_sg[k]])
            if s == 0:
                dve.op(lambda e: e.tensor_scalar(sg[k][:, 0:n], sg[k][:, 0:n], mask_col, None, ALU.mult),
                       reads=[r_sg[k], r_pvec], writes=[r_sg[k]])
            bi2 = brot.next()
            grp = PEGroup(pe, bank_res[bi2])
            for kc in range(KC):
                grp.mm(bank_ap[bi2][:, 0:n], wv[:, kc, jj * 128:(jj + 1) * 128], hT[:, kc, c0:c0 + n],
                       [rv] + hT_reads(s), last=(kc == KC - 1))
            dve.op(lambda e: e.scalar_tensor_tensor(u[:, j, c0:c0 + n], bank_ap[bi2][:, 0:n],
                                                    pvec[:, PV_BIN + j:PV_BIN + j + 1], sg[k][:, 0:n],
                                                    ALU.add, ALU.mult),
                   reads=[bank_res[bi2], r_pvec, r_sg[k]], writes=[r_u[j][s]])

        for j in range(8):
            if j in (0, 4):
                build_dg(j // 4)
            glu(j, 0)
            glu(j, 1)
            if j % 2 == 1 and nsteps:
                nsteps.pop(0)()
        while nsteps:
            nsteps.pop(0)()
        for j in range(8):
            glu(j, 2)
        for q_ in range(4):
            ws_release(gw[q_][2])

        cbuf = sb(OFF_CB, [128, KC, TB], F32, "cbuf")
        r_cb = [[newres("cb%d_%d" % (j, s), OFF_CB + (j * TB + s * 512) * 4, 2048) for s in range(NS)] for j in range(8)]
        cbf = [sb(OFF_TMP + 4096 + i * 1024, [128, 512], BF16, "cbf") for i in range(2)]
        r_cbf = [newres("cbf%d" % i, OFF_TMP + 4096 + i * 1024, 1024) for i in range(2)]
        csq = [sb(OFF_TMP + 6144 + i * 1024, [128, 512], BF16, "csq") for i in range(2)]
        r_csq = [newres("csq%d" % i, OFF_TMP + 6144 + i * 1024, 1024) for i in range(2)]
        brot = Rot(range(4))
        tmp_rot = Rot(range(2))
        sgrp1 = [PEGroup(pe, bank_res[4 + s]) for s in range(NS)]
        sgrp2 = [PEGroup(pe, bank_res[6 + s]) for s in range(NS)]
        pend = []

        def flush_stats():
            while pend:
                s_, t_, j_ = pend.pop(0)
                sgrp1[s_].mm(bank_ap[4 + s_], ones_b, cbf[t_], [r_const, r_cbf[t_]], last=(j_ == 7), mark=[r_cbf[t_]])
                sgrp2[s_].mm(bank_ap[6 + s_], ones_b, csq[t_], [r_const, r_csq[t_]], last=(j_ == 7), mark=[r_csq[t_]])

        for j in range(8):
            d = j % 2
            for s in range(NS):
                bi = brot.next()
                grp = PEGroup(pe, bank_res[bi])
                base = 2 + 512 * s
                for k in range(31):
                    grp.mm(bank_ap[bi], dg[d][:, k, :], u[:, j, base + k: base + k + 512],
                           [r_dg[d], r_u[j][s], r_u[j][s + 1]], last=(k == 30))
                if s == NS - 1 and j + 2 < 8:
                    build_dg(j + 2)
                flush_stats()
                cab = pvec[:, PV_CAB + j:PV_CAB + j + 1]
                act.op(lambda e: e.activation(cbuf[:, j, s * 512:(s + 1) * 512], bank_ap[bi], AF.Identity, bias=cab, scale=1.0),
                       reads=[bank_res[bi], r_pvec], writes=[r_cb[j][s]])
                t = tmp_rot.next()
                act.op(lambda e: e.activation(cbf[t], bank_ap[bi], AF.Identity, bias=cab, scale=1.0),
                       reads=[bank_res[bi], r_pvec], writes=[r_cbf[t]])
                act.op(lambda e: e.activation(csq[t], bank_ap[bi], AF.Square, bias=cab, scale=1.0),
                       reads=[bank_res[bi], r_pvec], writes=[r_csq[t]])
                pend.append((s, t, j))
            if b == 0:
                ada_tile(4 + j)
        flush_stats()
        if b == 0:
            mod_cols([(SH2, 3, False), (SC2, 4, True)], r_mod2)
            mod_gates()
            pool.op(lambda e: e.tensor_tensor(borow[0:1, :], rowstage[0:1, 0:1024], g1b[0:1, :], ALU.mult), reads=[r_rowstage, r_mod2, r_rows], writes=[r_rows])
            pool.op(lambda e: e.tensor_tensor(bdrow[0:1, :], rowstage[0:1, 1024:2048], g2b[0:1, :], ALU.mult), reads=[r_rowstage, r_mod2, r_rows], writes=[r_rows])

        u2 = sb(OFF_U2, [128, KC, TB], BF16, "u2")
        r_u2 = [[newres("u2_%d_%d" % (j, s), OFF_U2 + (j * TB + s * 512) * 2, 1024) for s in range(NS)] for j in range(8)]
        mean = [sb(OFF_MG + 4096 + s * 6144, [128, 512], F32, "mean") for s in range(NS)]
        var = [sb(OFF_MG + 4096 + s * 6144 + 2048, [128, 512], F32, "var") for s in range(NS)]
        rstd = [sb(OFF_MG + 4096 + s * 6144 + 4096, [128, 512], F32, "rstd") for s in range(NS)]
        r_lna = [newres("lna%d" % s, OFF_MG + 4096 + s * 6144, 6144) for s in range(NS)]
        t12 = [sb(OFF_MG + i * 2048, [128, 512], F32, "t12") for i in range(2)]
        r_t12 = [newres("t12_%d" % i, OFF_MG + i * 2048, 2048) for i in range(2)]
        t12_rot = Rot(range(2))
        def p2b_stats():
            for s in range(NS):
                dve.op(lambda e: e.tensor_scalar(mean[s], bank_ap[4 + s], 1.0 / D, None, ALU.mult),
                       reads=[bank_res[4 + s]], writes=[r_lna[s]])
                dve.op(lambda e: e.tensor_tensor(var[s], mean[s], mean[s], ALU.mult), reads=[r_lna[s]], writes=[r_lna[s]])
                dve.op(lambda e: e.scalar_tensor_tensor(var[s], bank_ap[6 + s], 1.0 / D, var[s], ALU.mult, ALU.subtract),
                       reads=[bank_res[6 + s], r_lna[s]], writes=[r_lna[s]])
                act.op(lambda e: e.activation(rstd[s], var[s], AF.Sqrt, bias=eps_t, scale=1.0),
                       reads=[r_lna[s], r_const], writes=[r_lna[s]])
                dve.op(lambda e: e.reciprocal(rstd[s], rstd[s]), reads=[r_lna[s]], writes=[r_lna[s]])

        p = u
        r_p = r_u
        bxt = sg
        r_bxt = r_sg
        brot = Rot(range(6))
        wtl = {}

        def p2c1(j):
            g, jj = divmod(j, 4)
            if jj == 0:
                wtl["x"] = ws_get()
                wtl["c"] = ws_get()
            wx, rx, _ = wtl["x"]
            wc, rc, _ = wtl["c"]
            for s, (c0, n) in enumerate(segs):
                bi = brot.next()
                grp = PEGroup(pe, bank_res[bi])
                for kc in range(KC):
                    grp.mm(bank_ap[bi][:, 0:n], wx[:, kc, jj * 128:(jj + 1) * 128], hT[:, kc, c0:c0 + n],
                           [rx] + hT_reads(s), last=(kc == KC - 1))
                k = sg_rot.next()
                act.op(lambda e: e.activation(bxt[k][:, 0:n], bank_ap[bi][:, 0:n], AF.Identity,
                                              bias=pvec[:, PV_BIN + 32 + j:PV_BIN + 33 + j], scale=1.0),
                       reads=[bank_res[bi], r_pvec], writes=[r_bxt[k]])
                if s == 0:
                    dve.op(lambda e: e.tensor_scalar(bxt[k][:, 0:n], bxt[k][:, 0:n], mask_col, None, ALU.mult),
                           reads=[r_bxt[k], r_pvec], writes=[r_bxt[k]])
                bi2 = brot.next()
                grp = PEGroup(pe, bank_res[bi2])
                for kc in range(KC):
                    grp.mm(bank_ap[bi2][:, 0:n], wc[:, kc, jj * 128:(jj + 1) * 128], hT[:, kc, c0:c0 + n],
                           [rc] + hT_reads(s), last=(kc == KC - 1))
                dve.op(lambda e: e.scalar_tensor_tensor(p[:, j, c0:c0 + n], bank_ap[bi2][:, 0:n],
                                                        pvec[:, PV_BIN + 24 + j:PV_BIN + 25 + j], bxt[k][:, 0:n],
                                                        ALU.add, ALU.mult),
                       reads=[bank_res[bi2], r_pvec, r_bxt[k]], writes=[r_p[j][s]])
            if jj == 3:
                ws_release(wtl["x"][2])
                ws_release(wtl["c"][2])

        def p2b_norm(s, j):
            a = t12_rot.next()
            dve.op(lambda e: e.tensor_tensor(t12[a], cbuf[:, j, s * 512:(s + 1) * 512], mean[s], ALU.subtract),
                   reads=[r_cb[j][s], r_lna[s]], writes=[r_t12[a]])
            dve.op(lambda e: e.tensor_tensor(t12[a], t12[a], rstd[s], ALU.mult),
                   reads=[r_t12[a], r_lna[s]], writes=[r_t12[a]])
            act.op(lambda e: e.activation(u2[:, j, s * 512:(s + 1) * 512], t12[a], AF.Silu,
                                          bias=pvec[:, PV_LAB + j:PV_LAB + j + 1], scale=pvec[:, PV_LAG + j:PV_LAG + j + 1]),
                   reads=[r_t12[a], r_pvec], writes=[r_u2[j][s]])

        brot = Rot(range(4))
        p2c1(0)
        p2c1(1)
        p2b_stats()
        brot = Rot(range(6))
        for j in range(2, 8):
            p2c1(j)
            for s in range(NS):
                p2b_norm(s, j - 2)
        for j in range(6, 8):
            for s in range(NS):
                p2b_norm(s, j)

        v = sb(OFF_V, [128, KC, TB], BF16, "v")
        r_v = [[newres("v_%d_%d" % (j, s), OFF_V + (j * TB + s * 512) * 2, 1024) for s in range(NS)] for j in range(8)]
        for g in range(2):
            wgb, rgb, igb = ws_get()
            for jj in range(4):
                j = 4 * g + jj
                for s in range(NS):
                    c0 = HALO + s * 512
                    bi = brot.next()
                    grp = PEGroup(pe, bank_res[bi])
                    for kc in range(KC):
                        grp.mm(bank_ap[bi], wgb[:, kc, jj * 128:(jj + 1) * 128], hT[:, kc, c0:c0 + 512],
                               [rgb] + hT_reads(s + 1), last=(kc == KC - 1))
                    k = sg_rot.next()
                    act.op(lambda e: e.activation(bxt[k], bank_ap[bi], AF.Identity,
                                                  bias=pvec[:, PV_BIN + 16 + j:PV_BIN + 17 + j], scale=1.0),
                           reads=[bank_res[bi], r_pvec], writes=[r_bxt[k]])
                    bi2 = brot.next()
                    grp = PEGroup(pe, bank_res[bi2])
                    for k3 in range(3):
                        grp.mm(bank_ap[bi2], dg3[:, j * 3 + k3, :], p[:, j, c0 - 2 + k3: c0 - 2 + k3 + 512],
                               [r_dg3, r_p[j][s], r_p[j][s + 1]], last=(k3 == 2))
                    dve.op(lambda e: e.tensor_tensor(v[:, j, s * 512:(s + 1) * 512], bank_ap[bi2], bxt[k], ALU.mult),
                           reads=[bank_res[bi2], r_bxt[k]], writes=[r_v[j][s]])
            ws_release(igb)

        m1 = sb(OFF_CB, [128, KC, TB], F32, "m1")
        r_m1 = r_cb
        mg = sb(OFF_MG, [128, KC, TB], BF16, "merged")
        r_mg = [[newres("mg_%d_%d" % (j, s), OFF_MG + (j * TB + s * 512) * 2, 1024) for s in range(NS)] for j in range(8)]
        tb_ = [sb(OFF_TMP + 4096 + i * 2048, [128, 512], F32, "tb") for i in range(2)]
        r_tb = [newres("tb%d" % i, OFF_TMP + 4096 + i * 2048, 2048) for i in range(2)]
        tb_rot = Rot(range(2))
        for pss in range(2):
            for g in range(2):
                wga, rga, iga = ws_get()
                wo_, ro_, io_ = ws_get()
                for jj in range(4):
                    j = 4 * g + jj
                    for s in range(NS):
                        c0 = HALO + s * 512
                        bi = brot.next()
                        grp = PEGroup(pe, bank_res[bi])
                        for kc in range(KC):
                            grp.mm(bank_ap[bi], wga[:, kc, jj * 128:(jj + 1) * 128], hT[:, kc, c0:c0 + 512],
                                   [rga] + hT_reads(s + 1), last=(kc == KC - 1))
                        k = sg_rot.next()
                        bcol = PV_BIN + (40 if pss == 0 else 48) + j
                        act.op(lambda e: e.activation(sg[k], bank_ap[bi], AF.Sigmoid, bias=pvec[:, bcol:bcol + 1], scale=1.0),
                               reads=[bank_res[bi], r_pvec], writes=[r_sg[k]])
                        bi2 = brot.next()
                        grp = PEGroup(pe, bank_res[bi2])
                        src, rsrc = (u2, r_u2) if pss == 0 else (v, r_v)
                        for kc in range(KC):
                            grp.mm(bank_ap[bi2], wo_[:, kc, jj * 128:(jj + 1) * 128], src[:, kc, s * 512:(s + 1) * 512],
                                   [ro_, rsrc[kc][s]], last=(kc == KC - 1))
                        if pss == 0:
                            dve.op(lambda e: e.scalar_tensor_tensor(m1[:, j, s * 512:(s + 1) * 512], bank_ap[bi2],
                                                                    pvec[:, PV_BAO + j:PV_BAO + j + 1], sg[k],
                                                                    ALU.add, ALU.mult),
                                   reads=[bank_res[bi2], r_pvec, r_sg[k]], writes=[r_m1[j][s]])
                        else:
                            t = tb_rot.next()
                            dve.op(lambda e: e.tensor_tensor(tb_[t], bank_ap[bi2], sg[k], ALU.mult),
                                   reads=[bank_res[bi2], r_sg[k]], writes=[r_tb[t]])
                            dve.op(lambda e: e.tensor_tensor(mg[:, j, s * 512:(s + 1) * 512], m1[:, j, s * 512:(s + 1) * 512],
                                                             tb_[t], ALU.add),
                                   reads=[r_m1[j][s], r_tb[t]], writes=[r_mg[j][s]])
                ws_release(iga)
                ws_release(io_)

        x1 = sb(OFF_X1, [128, NT, D], F32, "x1")
        r_x1 = [newres("x1_%d" % t, OFF_X1 + t * 4096, 4096) for t in range(NT)]
        xt = [sb(OFF_XT[i], [128, D], F32, "xt3") for i in range(3)]
        r_xt = [newres("xt3_%d" % i, OFF_XT[i], 4096) for i in range(3)]
        h2n = [sb(OFF_XN[i], [128, D], BF16, "h2n") for i in range(2)]
        r_h2n = [newres("h2n%d" % i, OFF_XN[i], 2048) for i in range(2)]
        slA = mkslots(4, OFF_TQ)
        slB = mkslots(4, OFF_TQ + 640)
        h2T = sb(OFF_HT, [128, KC, TW], BF16, "h2T")
        _g = new_group(fw, ["h2T%d%s" % (t, ab) for t in range(NT) for ab in "ab"], OFF_HT, OFF_HT + 16896)
        r_h2T = [(_g[2 * t], _g[2 * t + 1]) for t in range(NT)]
        wo0, rwo0, iwo0 = ws_get()
        wo1, rwo1, iwo1 = ws_get()
        wo = [(wo0, rwo0), (wo1, rwo1)]

        def p3_t0(t):
            r0 = row0 + HALO + t * 128
            sp.dma(xt[t % 3], xs[r0:r0 + 128, :], r_xt[t % 3], writes=[r_xt[t % 3]])

        p3bank = {}

        def p3_t1(t):
            bks = []
            for h in range(2):
                bi = brot.next()
                bks.append(bi)
                grp = PEGroup(pe, bank_res[bi])
                grp.mm(bank_ap[bi], ident32, xt[t % 3][:, h * 512:(h + 1) * 512], [r_ident32, r_xt[t % 3]])
                grp.mm(bank_ap[bi], e0, borow[:, h * 512:(h + 1) * 512], [r_const, r_rows])
                for kc in range(KC):
                    grp.mm(bank_ap[bi], mg[:, kc, t * 128:(t + 1) * 128], wo[h][0][:, kc, :],
                           [r_mg[kc][t // 4], wo[h][1]], last=(kc == KC - 1))
            p3bank[t] = bks
            sl = slA[t % 4]
            for h in range(2):
                dve.op(lambda e: e.bn_stats(sl["st"][:, h, :], bank_ap[bks[h]]), reads=[bank_res[bks[h]], sl["res"]], writes=[sl["res"]])
            dve.op(lambda e: e.bn_aggr(sl["mv"], sl["st"]), reads=[sl["res"]], writes=[sl["res"]])

        def p3_t2(t):
            ln_sqrt(128, slA[t % 4], eps2_t)

        def p3_t3(t):
            sl = slA[t % 4]
            bks = p3bank[t]
            ln_recip(128, sl, nmr=False)
            for h in range(2):
                dve.op(lambda e: e.scalar_tensor_tensor(x1[:, t, h * 512:(h + 1) * 512], bank_ap[bks[h]], sl["mv"][:, 0:1],
                                                        bc[:, h * 512:(h + 1) * 512], ALU.subtract, ALU.mult),
                       reads=[bank_res[bks[h]], sl["res"], r_bc], writes=[r_x1[t]])
            dve.op(lambda e: e.scalar_tensor_tensor(x1[:, t, :], x1[:, t, :], sl["sd"][:, 0:1], bc[:, 1024:2048], ALU.mult, ALU.add),
                   reads=[r_x1[t], sl["res"], r_bc], writes=[r_x1[t]])
            ln_stats_a(x1[:, t, :], 128, slB[t % 4], r_x1[t])

        def p3_t4(t):
            ln_sqrt(128, slB[t % 4], eps_t)

        def p3_t5(t):
            ln_recip(128, slB[t % 4])

        def p3_t6(t):
            sl = slB[t % 4]
            act.op(lambda e: e.activation(h2n[t % 2], x1[:, t, :], AF.Identity, bias=sl["nm"], scale=sl["sd"][:, 0:1]),
                   reads=[r_x1[t], sl["res"]], writes=[r_h2n[t % 2]])

        def p3_t7(t):
            transpose_evac(h2n[t % 2], 128, r_h2n[t % 2], h2T, r_h2T[t], HALO + t * 128, SC2, SH2, r_mod2)

        p4 = {}

        def p4_setup():
            p4["fT"] = sb(OFF_FT, [128, FC, TB], BF16, "fT")
            p4["r_fT"] = [[newres("fT_%d_%d" % (fc, s), OFF_FT + (fc * TB + s * 512) * 2, 1024) for s in range(NS)] for fc in range(FC)]
            p4["rr"] = [sb(OFF_XT[0] + i * 1024, [128, 512], BF16, "rr") for i in range(3)]
            p4["r_rr"] = [newres("rr%d" % i, OFF_XT[0] + i * 1024, 1024) for i in range(3)]
            p4["rot"] = Rot(range(3))

        def p4_group(wu, ru, g, jj, s):
            fc = 4 * g + jj
            c0 = HALO + s * 512
            bi = brot.next()
            grp = PEGroup(pe, bank_res[bi])
            for kc in range(KC):
                grp.mm(bank_ap[bi], wu[:, kc, jj * 128:(jj + 1) * 128], h2T[:, kc, c0:c0 + 512],
                       [ru] + [r for pr in r_h2T[s * 4:(s + 1) * 4] for r in pr], last=(kc == KC - 1))
            k = p4["rot"].next()
            rr, r_rr = p4["rr"], p4["r_rr"]
            act.op(lambda e: e.activation(rr[k], bank_ap[bi], AF.Relu, bias=pvec[:, PV_BUP + fc:PV_BUP + fc + 1], scale=1.0),
                   reads=[bank_res[bi], r_pvec], writes=[r_rr[k]])
            dve.op(lambda e: e.tensor_tensor(p4["fT"][:, fc, s * 512:(s + 1) * 512], rr[k], rr[k], ALU.mult),
                   reads=[r_rr[k]], writes=[p4["r_fT"][fc][s]])

        NHEAD = 4
        head = []
        stages = [p3_t0, lambda t: (p3_t1(t), p3_t2(t)), lambda t: (p3_t3(t), p3_t4(t)), lambda t: (p3_t5(t), p3_t6(t)), p3_t7]
        nsteps3 = NT + len(stages) - 1
        for step in range(nsteps3):
            for sidx in reversed(range(len(stages))):
                t = step - sidx
                if 0 <= t < NT:
                    stages[sidx](t)
            if step == NT:
                ws_release(iwo0)
                ws_release(iwo1)
                p4_setup()
            if step >= NT and len(head) < NHEAD:
                g = len(head)
                head.append(ws_get())
                for jj in range(4):
                    p4_group(head[g][0], head[g][1], g, jj, 0)
        while len(head) < NHEAD:
            g = len(head)
            head.append(ws_get())
            for jj in range(4):
                p4_group(head[g][0], head[g][1], g, jj, 0)
        for g in range(NHEAD):
            for jj in range(4):
                p4_group(head[g][0], head[g][1], g, jj, 1)
            ws_release(head[g][2])
        for g in range(NHEAD, 8):
            wu, ru, iu = ws_get()
            for jj in range(4):
                for s in range(NS):
                    p4_group(wu, ru, g, jj, s)
            ws_release(iu)
        fT, r_fT = p4["fT"], p4["r_fT"]

        sl5 = mkslots(8, OFF_TQ)

        def p5_resid(h, t):
            dve.op(lambda e: e.tensor_tensor(x1[:, t, h * 512:(h + 1) * 512], bank_ap[t], x1[:, t, h * 512:(h + 1) * 512], ALU.add),
                   reads=[bank_res[t], r_x1[t]], writes=[r_x1[t]])
            ln_stats_a(x1[:, t, :], 128, sl5[t], r_x1[t], halves=(h,), aggr=(h == 1))

        def p5_e1(t):
            ln_sqrt(128, sl5[t], eps2_t)

        def p5_e2(t):
            ln_recip(128, sl5[t], nmr=False)
            ln_affine_dve(x1[:, t, :], 128, sl5[t], r_x1[t], bc[:, 2048:3072], bc[:, 3072:4096])

        def p5_e3(t):
            r0 = b * TB + t * 128
            ev = sp.dma(y[r0:r0 + 128, :], x1[:, t, :], r_x1[t], reads=[r_x1[t]])
            out_events.append(ev)

        p5_stages = [None, p5_e1, p5_e2, p5_e3]

        def p5_tail_step(t):
            for k_ in (3, 2, 1):
                if 0 <= t - k_ < NT:
                    p5_stages[k_](t - k_)

        def mm_piece(grp, t, h, gq, wd, rd, mark_last):
            if gq == 0:
                grp.mm(bank_ap[t], e0, bdrow[:, h * 512:(h + 1) * 512], [r_const, r_rows])
            for kc in range(KC):
                fc = gq * 8 + kc
                lastmm = (gq == 3 and kc == KC - 1)
                mk = [rd] if (mark_last and kc == KC - 1 and not lastmm) else ()
                grp.mm(bank_ap[t], fT[:, fc, t * 128:(t + 1) * 128], wd[:, kc, :],
                       [r_fT[fc][t // 4], rd], last=lastmm, mark=mk)

        grps = [PEGroup(pe, bank_res[t]) for t in range(NT)]
        for gq in range(4):
            wd, rd, idn = ws_get()
            for t in range(NT):
                mm_piece(grps[t], t, 0, gq, wd, rd, mark_last=(t == NT - 1))
                if gq == 3:
                    p5_resid(0, t)
            ws_release(idn)
        ws_flush()
        grps = [PEGroup(pe, bank_res[t]) for t in range(NT)]
        for gq in range(1):
            wd, rd, idn = ws_get()
            for t in range(NT):
                mm_piece(grps[t], t, 1, gq, wd, rd, mark_last=(t == NT - 1))
            ws_release(idn)
        tl = [ws_get() for _ in range(3)]
        nsteps = []
        if b + 1 < NB:
            ctx, nsteps = p1_setup(b + 1, low_banks=True)
            for _ in range(3):
                nsteps.pop(0)()
        for t in range(NT):
            for q_ in range(3):
                mm_piece(grps[t], t, 1, 1 + q_, tl[q_][0], tl[q_][1], mark_last=False)
            p5_resid(1, t)
            p5_tail_step(t)
            if t >= 1 and nsteps:
                nsteps.pop(0)()
        for q_ in range(3):
            ws_release(tl[q_][2])
        for t in range(NT, NT + 3):
            p5_tail_step(t)

    for ev in out_events:
        sp.wait_ev(ev)
    if needed is None:
        return {k: sorted(v) for k, v in fw.rec.items()}
    return nc


_CACHE = {}


def kernel(x, c, w_ada, b_ada, w_in, b_in, conv_a_w, conv_a_b, ln_a_g, ln_a_b, w_a_out, b_a_out, conv_b_w, w_b_out,
           w_o, b_o, ln1_g, ln1_b, w_up, b_up, w_down, b_down, ln2_g, ln2_b):
    f = lambda a: np.ascontiguousarray(np.asarray(a, dtype=np.float32))
    x2 = f(x).reshape(SEQ, D)

    def pp(vec):
        vec = f(vec).reshape(-1, 128)
        return vec.T

    pv = np.zeros((128, NPV), np.float32)
    pv[:, PV_BIN:PV_BIN + 56] = pp(b_in)
    pv[:, PV_CAB:PV_CAB + 8] = pp(conv_a_b)
    pv[:, PV_LAG:PV_LAG + 8] = pp(ln_a_g)
    pv[:, PV_LAB:PV_LAB + 8] = pp(ln_a_b)
    pv[:, PV_BAO:PV_BAO + 8] = pp(b_a_out)
    pv[:, PV_BUP:PV_BUP + 32] = pp(b_up)
    caw = f(conv_a_w).reshape(31, 8, 128)
    pv[:, PV_CAW:PV_CAW + 248] = caw.transpose(2, 1, 0).reshape(128, 248)
    cbw = f(conv_b_w).reshape(3, 8, 128)
    pv[:, PV_CBW:PV_CBW + 24] = cbw.transpose(2, 1, 0).reshape(128, 24)
    pv[:, PV_C:PV_C + 8] = pp(c)
    rows = np.concatenate([f(b_ada).reshape(-1), f(b_o).reshape(-1), f(b_down).reshape(-1)])[None, :]
    bcv = np.concatenate([f(ln1_g).reshape(-1), f(ln1_b).reshape(-1), f(ln2_g).reshape(-1), f(ln2_b).reshape(-1)])
    bcm = np.ascontiguousarray(np.broadcast_to(bcv[None, :], (128, 4096)))
    wts = dict(w_ada=f(w_ada).reshape(D, 6 * D), w_in=f(w_in).reshape(D, 7 * D), w_a_out=f(w_a_out).reshape(D, D),
               w_b_out=f(w_b_out).reshape(D, D), w_o=f(w_o).reshape(D, D), w_up=f(w_up).reshape(D, DFF),
               w_down=f(w_down).reshape(DFF, D))
    in_maps = []
    for k in range(NCORES):
        xsk = np.zeros((TPC + HALO, D), np.float32)
        if k == 0:
            xsk[HALO:] = x2[0:TPC]
        else:
            xsk[:] = x2[k * TPC - HALO:(k + 1) * TPC]
        pvk = pv.copy()
        pvk[:, PV_MASK] = 0.0 if k == 0 else 1.0
        pvk[:, PV_MASK + 1] = 1.0
        m = dict(xs=xsk, pvec=pvk, rows=rows, bc=bcm)
        m.update(wts)
        in_maps.append(m)
    if "nc" not in _CACHE:
        rec = build_program(None)
        _CACHE["nc"] = build_program(rec)
    res = run_bass_kernel_spmd(_CACHE["nc"], in_maps, core_ids=list(range(NCORES)))
    out = np.concatenate([np.asarray(r["y"]) for r in res.results], axis=0)
    return out.reshape(1, SEQ, D).astype(np.float32)
```

```python
import bisect
import numpy as np
import concourse.bass as bass
import concourse.mybir as mybir
from concourse.bass_utils import run_bass_kernel_spmd

F32 = mybir.dt.float32
BF16 = mybir.dt.bfloat16
AF = mybir.ActivationFunctionType
ALU = mybir.AluOpType

NCORES = 8
D = 1024
KC = 8
SEQ = 16384
TPC = SEQ // NCORES
NB = 2
TB = TPC // NB
NS = TB // 512
NT = TB // 128
HALO = 32
TW = HALO + TB
DFF = 4096
FC = DFF // 128
ALPHA = float(2.0 ** 0.25)
EPS = 1e-5

PV_BIN, PV_CAB, PV_LAG, PV_LAB, PV_BAO, PV_BUP, PV_CAW, PV_CBW, PV_C, PV_MASK = 0, 56, 64, 72, 80, 88, 120, 368, 392, 400
NPV = 404


class Res:
    __slots__ = ("name", "w", "r", "dsem", "dcnt", "lo", "hi", "dead", "excl")

    def __init__(self, name, lo=None, hi=None):
        self.name = name
        self.w = None
        self.r = {}
        self.dsem = None
        self.dcnt = 0
        self.lo = lo
        self.hi = hi
        self.dead = False
        self.excl = False


class Eng:
    def __init__(self, fw, handle, name, selfsync=True):
        self.fw = fw
        self.h = handle
        self.name = name
        self.sem = fw.nc.alloc_semaphore("s_" + name)
        self.idx = 0
        self.cnt = 0
        self.seen = {}
        self.selfsync = selfsync

    def wait_ev(self, ev, same_ok=False):
        if ev is None:
            return
        sem, val, key = ev
        if key == self.name and (same_ok or not self.selfsync):
            return
        sid = id(sem)
        if self.seen.get(sid, 0) >= val:
            return
        self.h.wait_ge(sem, self.fw.sem_value(key, val))
        self.seen[sid] = val

    def deps(self, reads, writes):
        for r in reads:
            assert not r.dead, ("read of dead res", r.name)
            self.wait_ev(r.w)
            if r.excl:
                for ev in r.r.values():
                    self.wait_ev(ev, same_ok=True)
        for w in writes:
            assert not w.dead, ("write of dead res", w.name)
            self.wait_ev(w.w)
            for ev in w.r.values():
                self.wait_ev(ev)

    def mark(self, inst, reads=(), writes=()):
        self.idx += 1
        if self.fw.needs_inc(self.name, self.idx):
            self.cnt += 1
            inst.then_inc(self.sem, 1)
        ev = (self.sem, self.idx, self.name)
        for r in reads:
            r.r[self.name] = ev
        for w in writes:
            w.w = ev
            w.r = {}
        return ev

    def op(self, fn, reads=(), writes=()):
        self.deps(reads, writes)
        inst = fn(self.h)
        self.mark(inst, reads, writes)
        return inst

    def dma(self, out, in_, owner, reads=(), writes=()):
        self.deps(reads, writes)
        if owner.dsem is None:
            self.fw.nsem += 1
            owner.dsem = self.fw.nc.alloc_semaphore("d%d_%s" % (self.fw.nsem, owner.name))
        owner.dcnt += 16
        inst = self.h.dma_start(out=out, in_=in_)
        inst.then_inc(owner.dsem, 16)
        ev = (owner.dsem, owner.dcnt, "dma_" + owner.name)
        for r in reads:
            r.r["dma_%s_%d" % (owner.name, owner.dcnt)] = ev
        for w in writes:
            w.w = ev
            w.r = {}
        return ev


class PEGroup:
    def __init__(self, pe, bank):
        self.pe = pe
        self.bank = bank
        self.first = True
        self.reads = {}

    def mm(self, out, lhsT, rhs, reads, last=False, mark=()):
        pe = self.pe
        if self.first:
            pe.deps(reads, [self.bank])
        else:
            pe.deps(reads, [])
        inst = pe.h.matmul(out, lhsT, rhs, start=self.first, stop=last)
        self.first = False
        for r in reads:
            self.reads[id(r)] = r
        if last:
            pe.mark(inst, list(self.reads.values()), [self.bank])
            self.reads = {}
        elif mark:
            pe.mark(inst, list(mark), [])
            for r in mark:
                self.reads.pop(id(r), None)
        return inst


class FW:
    def __init__(self, nc, needed=None):
        self.nc = nc
        self.needed = needed
        self.rec = {}
        self.registry = []
        self.nsem = 0
        self.pe = Eng(self, nc.tensor, "pe", selfsync=False)
        self.act = Eng(self, nc.scalar, "act")
        self.dve = Eng(self, nc.vector, "dve")
        self.pool = Eng(self, nc.gpsimd, "pool")
        self.sp = Eng(self, nc.sync, "sp")

    def needs_inc(self, key, idx):
        if self.needed is None:
            return True
        lst = self.needed.get(key, [])
        i = bisect.bisect_left(lst, idx)
        return i < len(lst) and lst[i] == idx

    def sem_value(self, key, val):
        if key.startswith("dma_"):
            return val
        if self.needed is None:
            self.rec.setdefault(key, set()).add(val)
            return val
        lst = self.needed[key]
        i = bisect.bisect_left(lst, val)
        assert i < len(lst) and lst[i] == val, (key, val)
        return i + 1

    def new_res(self, name, lo, hi):
        r = Res(name, lo, hi)
        keep = []
        for o in self.registry:
            if o.lo < hi and lo < o.hi:
                if o.w is not None:
                    k = o.w[2] if not o.w[2].startswith("dma_") else o.w[2] + str(o.w[1])
                    if k not in r.r or r.r[k][1] < o.w[1]:
                        r.r[k] = o.w
                for k, ev in o.r.items():
                    if k not in r.r or r.r[k][1] < ev[1]:
                        r.r[k] = ev
                o.dead = True
                for glo, ghi in ((o.lo, lo), (hi, o.hi)):
                    if glo < ghi:
                        gh = Res(o.name + "~", glo, ghi)
                        gh.w = o.w
                        gh.r = dict(o.r)
                        gh.dead = True
                        keep.append(gh)
            else:
                keep.append(o)
        keep.append(r)
        self.registry = keep
        return r


def new_group(fw, names, lo, hi):
    base = fw.new_res("grp", lo, hi)
    fw.registry.remove(base)
    out = []
    for n in names:
        r = Res(n, lo, hi)
        r.r = dict(base.r)
        out.append(r)
    fw.registry.extend(out)
    return out


class Rot:
    def __init__(self, items):
        self.items = list(items)
        self.i = 0

    def next(self):
        x = self.items[self.i % len(self.items)]
        self.i += 1
        return x


def build_program(needed=None):
    nc = bass.Bass("TRN2", target_bir_lowering=False)
    fw = FW(nc, needed)
    pe, act, dve, pool, sp = fw.pe, fw.act, fw.dve, fw.pool, fw.sp

    def din(name, shape):
        return nc.dram_tensor(name, shape, F32, kind="ExternalInput").ap()

    xs = din("xs", [TPC + HALO, D])
    pvec_d = din("pvec", [128, NPV])
    rows_d = din("rows", [1, 8192])
    bc_d = din("bc", [128, 4096])
    w_ada = din("w_ada", [D, 6 * D])
    w_in = din("w_in", [D, 7 * D])
    w_a_out = din("w_a_out", [D, D])
    w_b_out = din("w_b_out", [D, D])
    w_o = din("w_o", [D, D])
    w_up = din("w_up", [D, DFF])
    w_down = din("w_down", [DFF, D])
    y = nc.dram_tensor("y", [TPC, D], F32, kind="ExternalOutput").ap()

    C_SZ = 38400
    W0 = C_SZ
    NWB = 4
    A0 = W0 + NWB * 8192
    A_SZ = 132096 + 4096
    TOTAL = A0 + A_SZ
    lo, hi = nc.bump_sbuf(TOTAL)
    cnt = [0]

    def sb(off, shape, dtype, name):
        cnt[0] += 1
        return nc.alloc_sbuf_tensor_at("%s_%d" % (name, cnt[0]), list(shape), dtype, offset=lo + off).ap()

    def newres(name, off, nbytes):
        return fw.new_res(name, off, off + nbytes)

    o = 0
    pvec = sb(o, [128, NPV], F32, "pvec"); o += 1664
    eps_t = sb(o, [128, 1], F32, "eps"); eps2_t = sb(o + 32, [128, 1], F32, "eps2"); o += 64
    ident = sb(o, [128, 128], BF16, "ident"); o += 256
    ones_b = sb(o, [128, 128], BF16, "onesb"); o += 256
    ones_f = sb(o, [1, 128], F32, "onesf"); o += 512
    dg3 = sb(o, [128, 24, 128], BF16, "dg3"); o += 6144
    borow = sb(o, [128, D], BF16, "borow"); o += 2048
    bdrow = sb(o, [128, D], BF16, "bdrow"); o += 2048
    g1b = sb(o, [128, D], F32, "g1b"); o += 4096
    g2b = sb(o, [128, D], F32, "g2b"); o += 4096
    bc = sb(o, [128, 4096], F32, "bc"); o += 16384
    modT = sb(o, [128, 32], F32, "modT"); o += 128
    cact = sb(o, [128, 8], BF16, "cact"); o += 64
    NST = 4
    st_sb, mv_sb, sd_sb, nm_sb, st_res = [], [], [], [], []
    for i in range(NST):
        st_sb.append(sb(o, [128, 2, 6], F32, "st")); o += 64
        mv_sb.append(sb(o, [128, 2], F32, "mv")); o += 32
        sd_sb.append(sb(o, [128, 1], F32, "sd")); o += 32
        nm_sb.append(sb(o, [128, 1], F32, "nm")); o += 32
        st_res.append(Res("st%d" % i))
    assert o <= C_SZ, o
    st_rot = Rot(range(NST))
    r_pvec, r_const, r_bc, r_rows, r_mod, r_dg3 = Res("pvec"), Res("const"), Res("bc"), Res("rows"), Res("mod"), Res("dg3")
    r_cact, r_ident = Res("cact"), Res("ident")

    wb_ap = [sb(W0 + i * 8192, [128, 8, 512], BF16, "wb") for i in range(NWB)]
    wb_res = [Res("wb%d" % i) for i in range(NWB)]

    bank_ap = [nc.alloc_psum_tensor("bank%d" % i, [128, 512], F32).ap() for i in range(8)]
    bank_res = [Res("bank%d" % i) for i in range(8)]
    for r_ in bank_res:
        r_.excl = True

    OFF_HT = A0
    OFF_XT = [A0 + 16896 + i * 4096 for i in range(3)]
    OFF_XN = [A0 + 29184 + i * 2048 for i in range(2)]
    OFF_TMP = A0 + 16896
    OFF_X1 = A0 + 33280
    OFF_FT = A0 + 66048
    OFF_U = A0 + 33280
    OFF_CB = A0 + 50176
    OFF_U2 = A0 + 82944
    OFF_V = A0 + 99328
    OFF_MG = A0 + 115712
    OFF_TQ = A0 + 132096

    def wsrc(w, r0, c0):
        return w.rearrange("(kc p) c -> p kc c", p=128)[:, r0 // 128:r0 // 128 + 8, c0:c0 + 512]

    wtiles = []
    wscale = {}
    for nt in range(4):
        wtiles.append(wsrc(w_ada, 0, nt * 512))
    for b in range(NB):
        for g in range(2):
            wtiles.append(wsrc(w_in, 0, 1024 + g * 512))
            wtiles.append(wsrc(w_in, 0, g * 512))
        if b == 0:
            for nt in range(4, 12):
                wtiles.append(wsrc(w_ada, 0, nt * 512))
        for g in range(2):
            wtiles.append(wsrc(w_in, 0, 4096 + g * 512))
            wtiles.append(wsrc(w_in, 0, 3072 + g * 512))
        for g in range(2):
            wtiles.append(wsrc(w_in, 0, 2048 + g * 512))
        for g in range(2):
            wtiles.append(wsrc(w_in, 0, 5120 + g * 512))
            wtiles.append(wsrc(w_a_out, 0, g * 512))
        for g in range(2):
            wtiles.append(wsrc(w_in, 0, 6144 + g * 512))
            wtiles.append(wsrc(w_b_out, 0, g * 512))
        for h in range(2):
            wscale[len(wtiles)] = ("g1", h)
            wtiles.append(wsrc(w_o, 0, h * 512))
        for g in range(8):
            wtiles.append(wsrc(w_up, 0, g * 512))
        for h in range(2):
            for gq in range(4):
                wscale[len(wtiles)] = ("g2", h)
                wtiles.append(wsrc(w_down, gq * 1024, h * 512))

    ws = {"issued": 0, "released": -1, "next": 0, "pending": None}

    def ws_prescale(i):
        which, h = wscale[i]
        gb = g1b if which == "g1" else g2b
        wt, wr = wb_ap[i % NWB], wb_res[i % NWB]
        pool.op(lambda e: e.tensor_tensor(wt, wt, gb[:, h * 512:(h + 1) * 512].unsqueeze(1).broadcast_to([128, 8, 512]), ALU.mult),
                reads=[wr, r_mod2], writes=[wr])

    def ws_flush():
        if ws["pending"] is not None:
            ws_prescale(ws["pending"])
            ws["pending"] = None

    def ws_pump():
        while ws["issued"] < len(wtiles) and ws["issued"] <= ws["released"] + NWB:
            i = ws["issued"]
            s = i % NWB
            pool.dma(wb_ap[s], wtiles[i], wb_res[s], writes=[wb_res[s]])
            ws["issued"] += 1
            ws_flush()
            if i in wscale:
                ws["pending"] = i

    def ws_get():
        i = ws["next"]
        ws["next"] += 1
        assert i < ws["issued"], "weight tile not yet issued"
        if ws["pending"] == i:
            ws_flush()
        return wb_ap[i % NWB], wb_res[i % NWB], i

    def ws_release(i):
        assert i == ws["released"] + 1, (i, ws["released"])
        ws["released"] = i
        ws_pump()

    sp.dma(pvec, pvec_d, r_pvec, writes=[r_pvec])
    ws_pump()
    mod_row = sb(OFF_V, [1, 6144], F32, "modrow")
    identf = sb(OFF_U2, [128, 128], F32, "identf")
    r_modrow = newres("modrow", OFF_V, 24576)
    r_identf = newres("identf", OFF_U2, 512)
    sp.dma(mod_row, rows_d[0:1, 0:6144], r_modrow, writes=[r_modrow])
    sp.dma(bc, bc_d, r_bc, writes=[r_bc])
    rowstage = sb(OFF_V + 24576, [1, 2048], F32, "rowstage")
    r_rowstage = newres("rowstage", OFF_V + 24576, 8192)
    sp.dma(rowstage, rows_d[0:1, 6144:8192], r_rowstage, writes=[r_rowstage])
    dve.op(lambda e: e.memset(eps_t, EPS), writes=[r_const])
    dve.op(lambda e: e.memset(eps2_t, EPS / (ALPHA * ALPHA)), writes=[r_const])
    dve.op(lambda e: e.memset(ones_b, 1.0), writes=[r_const])
    dve.op(lambda e: e.memset(ones_f, 1.0), writes=[r_const])
    pool.op(lambda e: e.memset(identf, 1.0), writes=[r_identf])
    pool.op(lambda e: e.affine_select(out=identf, in_=identf, pattern=[[-1, 128]], compare_op=ALU.is_equal,
                                      fill=0.0, base=0, channel_multiplier=1), reads=[r_identf], writes=[r_identf])
    dve.op(lambda e: e.tensor_copy(ident, identf), reads=[r_identf], writes=[r_ident])
    e0 = sb(OFF_TQ + 3584, [128, 128], BF16, "e0")
    dve.op(lambda e: e.memset(e0, 0.0), writes=[r_const])
    dve.op(lambda e: e.memset(e0[0:1, :], 1.0), writes=[r_const])
    dve.op(lambda e: e.memset(borow, 0.0), writes=[r_rows])
    dve.op(lambda e: e.memset(bdrow, 0.0), writes=[r_rows])
    zrow = sb(OFF_TQ + 2560, [1, 512], BF16, "zrow")
    dve.op(lambda e: e.memset(zrow, 0.0), writes=[r_const])
    ident32 = sb(OFF_TQ + 2048, [128, 128], F32, "ident32")
    r_ident32 = Res("ident32")
    dve.op(lambda e: e.tensor_copy(ident32, identf), reads=[r_identf], writes=[r_ident32])
    pool.op(lambda e: e.tensor_tensor(dg3, ident.unsqueeze(1).broadcast_to([128, 24, 128]),
                                      pvec[:, PV_CBW:PV_CBW + 24].unsqueeze(2).broadcast_to([128, 24, 128]), ALU.mult),
            reads=[r_ident, r_pvec], writes=[r_dg3])
    act.op(lambda e: e.activation(cact, pvec[:, PV_C:PV_C + 8], AF.Silu), reads=[r_pvec], writes=[r_cact])

    brot = Rot(range(6))
    SH1, SC1, SH2, SC2 = 0, 8, 16, 24
    r_mod1, r_mod2 = Res("mod1"), Res("mod2")

    def ada_tile(nt):
        wt, wr, wi = ws_get()
        bi = brot.next()
        g = PEGroup(pe, bank_res[bi])
        for kc in range(KC):
            g.mm(bank_ap[bi][0:1, :], cact[:, kc:kc + 1], wt[:, kc, :], [r_cact, wr], last=(kc == KC - 1))
        ws_release(wi)
        dve.op(lambda e: e.tensor_tensor(mod_row[0:1, nt * 512:(nt + 1) * 512], bank_ap[bi][0:1, :],
                                         mod_row[0:1, nt * 512:(nt + 1) * 512], ALU.add),
               reads=[bank_res[bi], r_modrow], writes=[r_modrow])

    def mod_cols(pairs, rres):
        bi = brot.next()
        for cb, sec, _ in pairs:
            for j in range(8):
                g = PEGroup(pe, bank_res[bi])
                g.mm(bank_ap[bi][:, cb + j:cb + j + 1], mod_row[0:1, sec * 1024 + j * 128: sec * 1024 + (j + 1) * 128],
                     ones_f[0:1, 0:1], [r_modrow, r_const])
                g.mm(bank_ap[bi][:, cb + j:cb + j + 1], ones_b[0:1, :], zrow[0:1, 0:1], [r_const], last=True)
        for cb, sec, one in pairs:
            if one:
                dve.op(lambda e: e.tensor_scalar(modT[:, cb:cb + 8], bank_ap[bi][:, cb:cb + 8], 1.0, None, ALU.add),
                       reads=[bank_res[bi]], writes=[rres])
            else:
                dve.op(lambda e: e.tensor_copy(modT[:, cb:cb + 8], bank_ap[bi][:, cb:cb + 8]), reads=[bank_res[bi]], writes=[rres])

    def mod_gates():
        for gb, sec in ((g1b, 2), (g2b, 5)):
            for h in range(2):
                bi = brot.next()
                g = PEGroup(pe, bank_res[bi])
                g.mm(bank_ap[bi], ones_f[0:1, 0:128], mod_row[0:1, sec * 1024 + h * 512: sec * 1024 + (h + 1) * 512],
                     [r_modrow, r_const])
                g.mm(bank_ap[bi], ones_b[0:1, :], zrow[0:1, :], [r_const], last=True)
                dve.op(lambda e: e.tensor_scalar(gb[:, h * 512:(h + 1) * 512], bank_ap[bi], 1.0, 1.0 / ALPHA, ALU.add, ALU.mult),
                       reads=[bank_res[bi]], writes=[r_mod2])

    def mkslots(n, base_off=None):
        out = []
        for i in range(n):
            off = base_off + i * 160
            out.append(dict(st=sb(off, [128, 2, 6], F32, "st"), mv=sb(off + 64, [128, 2], F32, "mv"),
                            sd=sb(off + 96, [128, 1], F32, "sd"), nm=sb(off + 128, [128, 1], F32, "nm"),
                            res=newres("stslot", off, 160)))
        return out

    cslots = [dict(st=st_sb[i], mv=mv_sb[i], sd=sd_sb[i], nm=nm_sb[i], res=st_res[i]) for i in range(NST)]

    def ln_stats_a(src, npart, sl, rsrc, halves=(0, 1), aggr=True):
        st, mv, res = sl["st"], sl["mv"], sl["res"]
        for hh in halves:
            dve.op(lambda e: e.bn_stats(st[:npart, hh, :], src[:npart, hh * 512:(hh + 1) * 512]), reads=[rsrc, res], writes=[res])
        if aggr:
            dve.op(lambda e: e.bn_aggr(mv[:npart, :], st[:npart, :, :]), reads=[res], writes=[res])

    def ln_sqrt(npart, sl, eps_ap):
        act.op(lambda e: e.activation(sl["sd"][:npart, :], sl["mv"][:npart, 1:2], AF.Sqrt, bias=eps_ap[:npart, :], scale=1.0),
               reads=[sl["res"], r_const], writes=[sl["res"]])

    def ln_recip(npart, sl, nmr=True):
        dve.op(lambda e: e.reciprocal(sl["sd"][:npart, :], sl["sd"][:npart, :]), reads=[sl["res"]], writes=[sl["res"]])
        if nmr:
            dve.op(lambda e: e.tensor_scalar(sl["nm"][:npart, :], sl["mv"][:npart, 0:1], sl["sd"][:npart, 0:1], -1.0, ALU.mult, ALU.mult),
                   reads=[sl["res"]], writes=[sl["res"]])

    def ln_affine_dve(dst, npart, sl, rdst, g_b, b_b):
        dve.op(lambda e: e.scalar_tensor_tensor(dst, dst, sl["mv"][:npart, 0:1], g_b, ALU.subtract, ALU.mult),
               reads=[rdst, sl["res"], r_bc], writes=[rdst])
        dve.op(lambda e: e.scalar_tensor_tensor(dst, dst, sl["sd"][:npart, 0:1], b_b, ALU.mult, ALU.add),
               reads=[rdst, sl["res"], r_bc], writes=[rdst])

    pst_rot = Rot((6, 7))

    def transpose_evac(src_bf, npart, rsrc, dst, rdst, col0, sc_col, sh_col, rmod, n_dve=0, bank=None):
        if bank is not None:
            ba = bb = bank
            n_dve = 0
        elif n_dve == 0:
            ba = bb = pst_rot.next()
        else:
            ba, bb = 6, 7
        psa = bank_ap[ba].bitcast(BF16).rearrange("p (k t) -> p k t", k=8)
        psb = bank_ap[bb].bitcast(BF16).rearrange("p (k t) -> p k t", k=8)
        wr_banks = [bank_res[ba]] if ba == bb else [bank_res[ba], bank_res[bb]]
        pe.deps([rsrc, r_ident], wr_banks)
        inst = None
        for kc in range(KC):
            ps = psb if kc >= KC - n_dve else psa
            inst = pe.h.transpose(ps[:, kc, 0:npart], src_bf[:npart, kc * 128:(kc + 1) * 128], ident[:npart, :npart])
        pe.mark(inst, [rsrc, r_ident], wr_banks)
        for kc in range(KC):
            if sc_col is None:
                if kc >= KC - n_dve:
                    dve.op(lambda e: e.tensor_copy(dst[:, kc, col0:col0 + npart], psb[:, kc, 0:npart]),
                           reads=[bank_res[bb]], writes=[rdst[1]])
                else:
                    act.op(lambda e: e.activation(dst[:, kc, col0:col0 + npart], psa[:, kc, 0:npart], AF.Identity),
                           reads=[bank_res[ba]], writes=[rdst[0]])
            elif kc >= KC - n_dve:
                dve.op(lambda e: e.tensor_scalar(dst[:, kc, col0:col0 + npart], psb[:, kc, 0:npart],
                                                 modT[:, sc_col + kc:sc_col + kc + 1], modT[:, sh_col + kc:sh_col + kc + 1],
                                                 ALU.mult, ALU.add),
                       reads=[bank_res[bb], rmod], writes=[rdst[1]])
            else:
                act.op(lambda e: e.activation(dst[:, kc, col0:col0 + npart], psa[:, kc, 0:npart], AF.Identity,
                                              bias=modT[:, sh_col + kc:sh_col + kc + 1], scale=modT[:, sc_col + kc:sc_col + kc + 1]),
                       reads=[bank_res[ba], rmod], writes=[rdst[0]])

    out_events = []

    def p1_setup(b, low_banks=False, raw=False):
        row0 = b * TB
        hT = sb(OFF_HT, [128, KC, TW], BF16, "hT")
        _g = new_group(fw, ["hT%d%s" % (t, ab) for t in range(NT + 1) for ab in "ab"], OFF_HT, OFF_HT + 16896)
        r_hT = [(_g[2 * t], _g[2 * t + 1]) for t in range(NT + 1)]
        xt = [sb(OFF_XT[i], [128, D], F32, "xt") for i in range(3)]
        r_xt = [newres("xt%d" % i, OFF_XT[i], 4096) for i in range(3)]
        xn = [sb(OFF_XN[i], [128, D], BF16, "xn") for i in range(2)]
        r_xn = [newres("xn%d" % i, OFF_XN[i], 2048) for i in range(2)]
        tiles = [("h", HALO, row0, 0)] + [(t, 128, row0 + HALO + t * 128, HALO + t * 128) for t in range(NT)]
        nti = len(tiles)
        sis = {}

        def p1_m0(i):
            _, npart, r0, _ = tiles[i]
            sp.dma(xt[i % 3][:npart, :], xs[r0:r0 + npart, :], r_xt[i % 3], writes=[r_xt[i % 3]])

        def p1_a(i):
            _, npart, _, _ = tiles[i]
            sis[i] = cslots[st_rot.next()]
            ln_stats_a(xt[i % 3], npart, sis[i], r_xt[i % 3])

        def p1_b(i):
            _, npart, _, _ = tiles[i]
            ln_sqrt(npart, sis[i], eps_t)

        def p1_c(i):
            _, npart, _, _ = tiles[i]
            sl = sis[i]
            if low_banks:
                ln_recip(npart, sl, nmr=True)
                act.op(lambda e: e.activation(xn[i % 2][:npart, :], xt[i % 3][:npart, :], AF.Identity,
                                              bias=sl["nm"][:npart, :], scale=sl["sd"][:npart, 0:1]),
                       reads=[r_xt[i % 3], sl["res"]], writes=[r_xn[i % 2]])
            else:
                ln_recip(npart, sl, nmr=False)
                dve.op(lambda e: e.tensor_scalar(xn[i % 2][:npart, :], xt[i % 3][:npart, :], sl["mv"][:npart, 0:1], sl["sd"][:npart, 0:1],
                                                 ALU.subtract, ALU.mult),
                       reads=[r_xt[i % 3], sl["res"]], writes=[r_xn[i % 2]])

        def p1_m2(i):
            _, npart, _, c0 = tiles[i]
            if low_banks:
                transpose_evac(xn[i % 2], npart, r_xn[i % 2], hT, r_hT[i], c0, SC1, SH1, r_mod1, bank=i % 2)
            elif raw:
                transpose_evac(xn[i % 2], npart, r_xn[i % 2], hT, r_hT[i], c0, None, None, None, n_dve=3)
            else:
                transpose_evac(xn[i % 2], npart, r_xn[i % 2], hT, r_hT[i], c0, SC1, SH1, r_mod1, n_dve=3)

        stages = [p1_m0, lambda i: (p1_a(i), p1_b(i)), p1_c, p1_m2]
        steps = []
        for step in range(nti + len(stages) - 1):
            def run(step=step):
                for sidx in reversed(range(len(stages))):
                    i = step - sidx
                    if 0 <= i < nti:
                        stages[sidx](i)
            steps.append(run)
        return dict(hT=hT, r_hT=r_hT, tiles=tiles), steps

    ctx, steps = p1_setup(0, raw=True)
    nada = 0
    for k_, st_ in enumerate(steps):
        st_()
        if k_ >= 3 and nada < 4:
            ada_tile(nada)
            nada += 1
    while nada < 4:
        ada_tile(nada)
        nada += 1
    mod_cols([(SH1, 0, False), (SC1, 1, True)], r_mod1)
    for i_, (_, npart_, _, c0_) in enumerate(ctx["tiles"]):
        ra_, rb_ = ctx["r_hT"][i_]
        for kc in range(KC):
            view = ctx["hT"][:, kc, c0_:c0_ + npart_]
            if kc < 5:
                dve.op(lambda e: e.tensor_scalar(view, view, modT[:, SC1 + kc:SC1 + kc + 1], modT[:, SH1 + kc:SH1 + kc + 1],
                                                 ALU.mult, ALU.add),
                       reads=[ra_, rb_, r_mod1], writes=[ra_])
            else:
                act.op(lambda e: e.activation(view, view, AF.Identity, bias=modT[:, SH1 + kc:SH1 + kc + 1],
                                              scale=modT[:, SC1 + kc:SC1 + kc + 1]),
                       reads=[ra_, rb_, r_mod1], writes=[rb_])

    nsteps = []
    for b in range(NB):
        row0 = b * TB
        hT, r_hT = ctx["hT"], ctx["r_hT"]

        def hT_reads(seg):
            tl_ = [r_hT[0]] if seg == 0 else r_hT[1 + (seg - 1) * 4: 1 + seg * 4]
            return [r for pr in tl_ for r in pr]

        segs = [(0, HALO), (HALO, 512), (HALO + 512, 512)]

        u = sb(OFF_U, [128, KC, TW], BF16, "u")
        r_u = [[newres("u%d_%d" % (j, s), OFF_U + (j * TW + segs[s][0]) * 2, segs[s][1] * 2) for s in range(3)] for j in range(8)]
        sg = [sb(OFF_TMP + i * 2048, [128, 512], F32, "sg") for i in range(2)]
        r_sg = [newres("sg%d" % i, OFF_TMP + i * 2048, 2048) for i in range(2)]
        sg_rot = Rot(range(2))
        brot = Rot(range(6))
        mask_col = pvec[:, PV_MASK + b:PV_MASK + b + 1]
        dg = [sb(OFF_U2 + i * 7936, [128, 31, 128], BF16, "dg31") for i in range(2)]
        r_dg = [newres("dg31_%d" % i, OFF_U2 + i * 7936, 7936) for i in range(2)]

        def build_dg(j_):
            dve.op(lambda e: e.tensor_tensor(dg[j_ % 2], ident.unsqueeze(1).broadcast_to([128, 31, 128]),
                                             pvec[:, PV_CAW + j_ * 31:PV_CAW + (j_ + 1) * 31].unsqueeze(2).broadcast_to([128, 31, 128]),
                                             ALU.mult),
                   reads=[r_ident, r_pvec], writes=[r_dg[j_ % 2]])

        gw = [ws_get() for _ in range(4)]

        def glu(j, s):
            g, jj = divmod(j, 4)
            (wg, rg, _), (wv, rv, _) = gw[2 * g], gw[2 * g + 1]
            c0, n = segs[s]
            bi = brot.next()
            grp = PEGroup(pe, bank_res[bi])
            for kc in range(KC):
                grp.mm(bank_ap[bi][:, 0:n], wg[:, kc, jj * 128:(jj + 1) * 128], hT[:, kc, c0:c0 + n],
                       [rg] + hT_reads(s), last=(kc == KC - 1))
            k = sg_rot.next()
            act.op(lambda e: e.activation(sg[k][:, 0:n], bank_ap[bi][:, 0:n], AF.Sigmoid,
                                          bias=pvec[:, PV_BIN + 8 + j:PV_BIN + 9 + j], scale=1.0),
                   reads=[bank_res[bi], r_pvec], writes=[r_sg[k]])
            if s == 0:
                dve.op(lambda e: e.tensor_scalar(sg[k][:, 0:n], sg[k][:, 0:n], mask_col, None, ALU.mult),
                       reads=[r_sg[k], r_pvec], writes=[r_sg[k]])
            bi2 = brot.next()
            grp = PEGroup(pe, bank_res[bi2])
            for kc in range(KC):
                grp.mm(bank_ap[bi2][:, 0:n], wv[:, kc, jj * 128:(jj + 1) * 128], hT[:, kc, c0:c0 + n],
                       [rv] + hT_reads(s), last=(kc == KC - 1))
            dve.op(lambda e: e.scalar_tensor_tensor(u[:, j, c0:c0 + n], bank_ap[bi2][:, 0:n],
                                                    pvec[:, PV_BIN + j:PV_BIN + j + 1], sg[k][:, 0:n],
                                                    ALU.add, ALU.mult),
                   reads=[bank_res[bi2], r_pvec, r_sg[k]], writes=[r_u[j][s]])

        for j in range(8):
            if j in (0, 4):
                build_dg(j // 4)
            glu(j, 0)
            glu(j, 1)
            if j % 2 == 1 and nsteps:
                nsteps.pop(0)()
        while nsteps:
            nsteps.pop(0)()
        for j in range(8):
            glu(j, 2)
        for q_ in range(4):
            ws_release(gw[q_][2])

        cbuf = sb(OFF_CB, [128, KC, TB], F32, "cbuf")
        r_cb = [[newres("cb%d_%d" % (j, s), OFF_CB + (j * TB + s * 512) * 4, 2048) for s in range(NS)] for j in range(8)]
        cbf = [sb(OFF_TMP + 4096 + i * 1024, [128, 512], BF16, "cbf") for i in range(2)]
        r_cbf = [newres("cbf%d" % i, OFF_TMP + 4096 + i * 1024, 1024) for i in range(2)]
        csq = [sb(OFF_TMP + 6144 + i * 1024, [128, 512], BF16, "csq") for i in range(2)]
        r_csq = [newres("csq%d" % i, OFF_TMP + 6144 + i * 1024, 1024) for i in range(2)]
        brot = Rot(range(4))
        tmp_rot = Rot(range(2))
        sgrp1 = [PEGroup(pe, bank_res[4 + s]) for s in range(NS)]
        sgrp2 = [PEGroup(pe, bank_res[6 + s]) for s in range(NS)]
        pend = []

        def flush_stats():
            while pend:
                s_, t_, j_ = pend.pop(0)
                sgrp1[s_].mm(bank_ap[4 + s_], ones_b, cbf[t_], [r_const, r_cbf[t_]], last=(j_ == 7), mark=[r_cbf[t_]])
                sgrp2[s_].mm(bank_ap[6 + s_], ones_b, csq[t_], [r_const, r_csq[t_]], last=(j_ == 7), mark=[r_csq[t_]])

        for j in range(8):
            d = j % 2
            for s in range(NS):
                bi = brot.next()
                grp = PEGroup(pe, bank_res[bi])
                base = 2 + 512 * s
                for k in range(31):
                    grp.mm(bank_ap[bi], dg[d][:, k, :], u[:, j, base + k: base + k + 512],
                           [r_dg[d], r_u[j][s], r_u[j][s + 1]], last=(k == 30))
                if s == NS - 1 and j + 2 < 8:
                    build_dg(j + 2)
                flush_stats()
                cab = pvec[:, PV_CAB + j:PV_CAB + j + 1]
                act.op(lambda e: e.activation(cbuf[:, j, s * 512:(s + 1) * 512], bank_ap[bi], AF.Identity, bias=cab, scale=1.0),
                       reads=[bank_res[bi], r_pvec], writes=[r_cb[j][s]])
                t = tmp_rot.next()
                act.op(lambda e: e.activation(cbf[t], bank_ap[bi], AF.Identity, bias=cab, scale=1.0),
                       reads=[bank_res[bi], r_pvec], writes=[r_cbf[t]])
                act.op(lambda e: e.activation(csq[t], bank_ap[bi], AF.Square, bias=cab, scale=1.0),
                       reads=[bank_res[bi], r_pvec], writes=[r_csq[t]])
                pend.append((s, t, j))
            if b == 0:
                ada_tile(4 + j)
        flush_stats()
        if b == 0:
            mod_cols([(SH2, 3, False), (SC2, 4, True)], r_mod2)
            mod_gates()
            pool.op(lambda e: e.tensor_tensor(borow[0:1, :], rowstage[0:1, 0:1024], g1b[0:1, :], ALU.mult), reads=[r_rowstage, r_mod2, r_rows], writes=[r_rows])
            pool.op(lambda e: e.tensor_tensor(bdrow[0:1, :], rowstage[0:1, 1024:2048], g2b[0:1, :], ALU.mult), reads=[r_rowstage, r_mod2, r_rows], writes=[r_rows])

        u2 = sb(OFF_U2, [128, KC, TB], BF16, "u2")
        r_u2 = [[newres("u2_%d_%d" % (j, s), OFF_U2 + (j * TB + s * 512) * 2, 1024) for s in range(NS)] for j in range(8)]
        mean = [sb(OFF_MG + 4096 + s * 6144, [128, 512], F32, "mean") for s in range(NS)]
        var = [sb(OFF_MG + 4096 + s * 6144 + 2048, [128, 512], F32, "var") for s in range(NS)]
        rstd = [sb(OFF_MG + 4096 + s * 6144 + 4096, [128, 512], F32, "rstd") for s in range(NS)]
        r_lna = [newres("lna%d" % s, OFF_MG + 4096 + s * 6144, 6144) for s in range(NS)]
        t12 = [sb(OFF_MG + i * 2048, [128, 512], F32, "t12") for i in range(2)]
        r_t12 = [newres("t12_%d" % i, OFF_MG + i * 2048, 2048) for i in range(2)]
        t12_rot = Rot(range(2))
        def p2b_stats():
            for s in range(NS):
                dve.op(lambda e: e.tensor_scalar(mean[s], bank_ap[4 + s], 1.0 / D, None, ALU.mult),
                       reads=[bank_res[4 + s]], writes=[r_lna[s]])
                dve.op(lambda e: e.tensor_tensor(var[s], mean[s], mean[s], ALU.mult), reads=[r_lna[s]], writes=[r_lna[s]])
                dve.op(lambda e: e.scalar_tensor_tensor(var[s], bank_ap[6 + s], 1.0 / D, var[s], ALU.mult, ALU.subtract),
                       reads=[bank_res[6 + s], r_lna[s]], writes=[r_lna[s]])
                act.op(lambda e: e.activation(rstd[s], var[s], AF.Sqrt, bias=eps_t, scale=1.0),
                       reads=[r_lna[s], r_const], writes=[r_lna[s]])
                dve.op(lambda e: e.reciprocal(rstd[s], rstd[s]), reads=[r_lna[s]], writes=[r_lna[s]])

        p = u
        r_p = r_u
        bxt = sg
        r_bxt = r_sg
        brot = Rot(range(6))
        wtl = {}

        def p2c1(j):
            g, jj = divmod(j, 4)
            if jj == 0:
                wtl["x"] = ws_get()
                wtl["c"] = ws_get()
            wx, rx, _ = wtl["x"]
            wc, rc, _ = wtl["c"]
            for s, (c0, n) in enumerate(segs):
                bi = brot.next()
                grp = PEGroup(pe, bank_res[bi])
                for kc in range(KC):
                    grp.mm(bank_ap[bi][:, 0:n], wx[:, kc, jj * 128:(jj + 1) * 128], hT[:, kc, c0:c0 + n],
                           [rx] + hT_reads(s), last=(kc == KC - 1))
                k = sg_rot.next()
                act.op(lambda e: e.activation(bxt[k][:, 0:n], bank_ap[bi][:, 0:n], AF.Identity,
                                              bias=pvec[:, PV_BIN + 32 + j:PV_BIN + 33 + j], scale=1.0),
                       reads=[bank_res[bi], r_pvec], writes=[r_bxt[k]])
                if s == 0:
                    dve.op(lambda e: e.tensor_scalar(bxt[k][:, 0:n], bxt[k][:, 0:n], mask_col, None, ALU.mult),
                           reads=[r_bxt[k], r_pvec], writes=[r_bxt[k]])
                bi2 = brot.next()
                grp = PEGroup(pe, bank_res[bi2])
                for kc in range(KC):
                    grp.mm(bank_ap[bi2][:, 0:n], wc[:, kc, jj * 128:(jj + 1) * 128], hT[:, kc, c0:c0 + n],
                           [rc] + hT_reads(s), last=(kc == KC - 1))
                dve.op(lambda e: e.scalar_tensor_tensor(p[:, j, c0:c0 + n], bank_ap[bi2][:, 0:n],
                                                        pvec[:, PV_BIN + 24 + j:PV_BIN + 25 + j], bxt[k][:, 0:n],
                                                        ALU.add, ALU.mult),
                       reads=[bank_res[bi2], r_pvec, r_bxt[k]], writes=[r_p[j][s]])
            if jj == 3:
                ws_release(wtl["x"][2])
                ws_release(wtl["c"][2])

        def p2b_norm(s, j):
            a = t12_rot.next()
            dve.op(lambda e: e.tensor_tensor(t12[a], cbuf[:, j, s * 512:(s + 1) * 512], mean[s], ALU.subtract),
                   reads=[r_cb[j][s], r_lna[s]], writes=[r_t12[a]])
            dve.op(lambda e: e.tensor_tensor(t12[a], t12[a], rstd[s], ALU.mult),
                   reads=[r_t12[a], r_lna[s]], writes=[r_t12[a]])
            act.op(lambda e: e.activation(u2[:, j, s * 512:(s + 1) * 512], t12[a], AF.Silu,
                                          bias=pvec[:, PV_LAB + j:PV_LAB + j + 1], scale=pvec[:, PV_LAG + j:PV_LAG + j + 1]),
                   reads=[r_t12[a], r_pvec], writes=[r_u2[j][s]])

        brot = Rot(range(4))
        p2c1(0)
        p2c1(1)
        p2b_stats()
        brot = Rot(range(6))
        for j in range(2, 8):
            p2c1(j)
            for s in range(NS):
                p2b_norm(s, j - 2)
        for j in range(6, 8):
            for s in range(NS):
                p2b_norm(s, j)

        v = sb(OFF_V, [128, KC, TB], BF16, "v")
        r_v = [[newres("v_%d_%d" % (j, s), OFF_V + (j * TB + s * 512) * 2, 1024) for s in range(NS)] for j in range(8)]
        for g in range(2):
            wgb, rgb, igb = ws_get()
            for jj in range(4):
                j = 4 * g + jj
                for s in range(NS):
                    c0 = HALO + s * 512
                    bi = brot.next()
                    grp = PEGroup(pe, bank_res[bi])
                    for kc in range(KC):
                        grp.mm(bank_ap[bi], wgb[:, kc, jj * 128:(jj + 1) * 128], hT[:, kc, c0:c0 + 512],
                               [rgb] + hT_reads(s + 1), last=(kc == KC - 1))
                    k = sg_rot.next()
                    act.op(lambda e: e.activation(bxt[k], bank_ap[bi], AF.Identity,
                                                  bias=pvec[:, PV_BIN + 16 + j:PV_BIN + 17 + j], scale=1.0),
                           reads=[bank_res[bi], r_pvec], writes=[r_bxt[k]])
                    bi2 = brot.next()
                    grp = PEGroup(pe, bank_res[bi2])
                    for k3 in range(3):
                        grp.mm(bank_ap[bi2], dg3[:, j * 3 + k3, :], p[:, j, c0 - 2 + k3: c0 - 2 + k3 + 512],
                               [r_dg3, r_p[j][s], r_p[j][s + 1]], last=(k3 == 2))
                    dve.op(lambda e: e.tensor_tensor(v[:, j, s * 512:(s + 1) * 512], bank_ap[bi2], bxt[k], ALU.mult),
                           reads=[bank_res[bi2], r_bxt[k]], writes=[r_v[j][s]])
            ws_release(igb)

        m1 = sb(OFF_CB, [128, KC, TB], F32, "m1")
        r_m1 = r_cb
        mg = sb(OFF_MG, [128, KC, TB], BF16, "merged")
        r_mg = [[newres("mg_%d_%d" % (j, s), OFF_MG + (j * TB + s * 512) * 2, 1024) for s in range(NS)] for j in range(8)]
        tb_ = [sb(OFF_TMP + 4096 + i * 2048, [128, 512], F32, "tb") for i in range(2)]
        r_tb = [newres("tb%d" % i, OFF_TMP + 4096 + i * 2048, 2048) for i in range(2)]
        tb_rot = Rot(range(2))
        for pss in range(2):
            for g in range(2):
                wga, rga, iga = ws_get()
                wo_, ro_, io_ = ws_get()
                for jj in range(4):
                    j = 4 * g + jj
                    for s in range(NS):
                        c0 = HALO + s * 512
                        bi = brot.next()
                        grp = PEGroup(pe, bank_res[bi])
                        for kc in range(KC):
                            grp.mm(bank_ap[bi], wga[:, kc, jj * 128:(jj + 1) * 128], hT[:, kc, c0:c0 + 512],
                                   [rga] + hT_reads(s + 1), last=(kc == KC - 1))
                        k = sg_rot.next()
                        bcol = PV_BIN + (40 if pss == 0 else 48) + j
                        act.op(lambda e: e.activation(sg[k], bank_ap[bi], AF.Sigmoid, bias=pvec[:, bcol:bcol + 1], scale=1.0),
                               reads=[bank_res[bi], r_pvec], writes=[r_sg[k]])
                        bi2 = brot.next()
                        grp = PEGroup(pe, bank_res[bi2])
                        src, rsrc = (u2, r_u2) if pss == 0 else (v, r_v)
                        for kc in range(KC):
                            grp.mm(bank_ap[bi2], wo_[:, kc, jj * 128:(jj + 1) * 128], src[:, kc, s * 512:(s + 1) * 512],
                                   [ro_, rsrc[kc][s]], last=(kc == KC - 1))
                        if pss == 0:
                            dve.op(lambda e: e.scalar_tensor_tensor(m1[:, j, s * 512:(s + 1) * 512], bank_ap[bi2],
                                                                    pvec[:, PV_BAO + j:PV_BAO + j + 1], sg[k],
                                                                    ALU.add, ALU.mult),
                                   reads=[bank_res[bi2], r_pvec, r_sg[k]], writes=[r_m1[j][s]])
                        else:
                            t = tb_rot.next()
                            dve.op(lambda e: e.tensor_tensor(tb_[t], bank_ap[bi2], sg[k], ALU.mult),
                                   reads=[bank_res[bi2], r_sg[k]], writes=[r_tb[t]])
                            dve.op(lambda e: e.tensor_tensor(mg[:, j, s * 512:(s + 1) * 512], m1[:, j, s * 512:(s + 1) * 512],
                                                             tb_[t], ALU.add),
                                   reads=[r_m1[j][s], r_tb[t]], writes=[r_mg[j][s]])
                ws_release(iga)
                ws_release(io_)

        x1 = sb(OFF_X1, [128, NT, D], F32, "x1")
        r_x1 = [newres("x1_%d" % t, OFF_X1 + t * 4096, 4096) for t in range(NT)]
        xt = [sb(OFF_XT[i], [128, D], F32, "xt3") for i in range(3)]
        r_xt = [newres("xt3_%d" % i, OFF_XT[i], 4096) for i in range(3)]
        h2n = [sb(OFF_XN[i], [128, D], BF16, "h2n") for i in range(2)]
        r_h2n = [newres("h2n%d" % i, OFF_XN[i], 2048) for i in range(2)]
        slA = mkslots(4, OFF_TQ)
        slB = mkslots(4, OFF_TQ + 640)
        h2T = sb(OFF_HT, [128, KC, TW], BF16, "h2T")
        _g = new_group(fw, ["h2T%d%s" % (t, ab) for t in range(NT) for ab in "ab"], OFF_HT, OFF_HT + 16896)
        r_h2T = [(_g[2 * t], _g[2 * t + 1]) for t in range(NT)]
        wo0, rwo0, iwo0 = ws_get()
        wo1, rwo1, iwo1 = ws_get()
        wo = [(wo0, rwo0), (wo1, rwo1)]

        def p3_t0(t):
            r0 = row0 + HALO + t * 128
            sp.dma(xt[t % 3], xs[r0:r0 + 128, :], r_xt[t % 3], writes=[r_xt[t % 3]])

        p3bank = {}

        def p3_t1(t):
            bks = []
            for h in range(2):
                bi = brot.next()
                bks.append(bi)
                grp = PEGroup(pe, bank_res[bi])
                grp.mm(bank_ap[bi], ident32, xt[t % 3][:, h * 512:(h + 1) * 512], [r_ident32, r_xt[t % 3]])
                grp.mm(bank_ap[bi], e0, borow[:, h * 512:(h + 1) * 512], [r_const, r_rows])
                for kc in range(KC):
                    grp.mm(bank_ap[bi], mg[:, kc, t * 128:(t + 1) * 128], wo[h][0][:, kc, :],
                           [r_mg[kc][t // 4], wo[h][1]], last=(kc == KC - 1))
            p3bank[t] = bks
            sl = slA[t % 4]
            for h in range(2):
                dve.op(lambda e: e.bn_stats(sl["st"][:, h, :], bank_ap[bks[h]]), reads=[bank_res[bks[h]], sl["res"]], writes=[sl["res"]])
            dve.op(lambda e: e.bn_aggr(sl["mv"], sl["st"]), reads=[sl["res"]], writes=[sl["res"]])

        def p3_t2(t):
            ln_sqrt(128, slA[t % 4], eps2_t)

        def p3_t3(t):
            sl = slA[t % 4]
            bks = p3bank[t]
            ln_recip(128, sl, nmr=False)
            for h in range(2):
                dve.op(lambda e: e.scalar_tensor_tensor(x1[:, t, h * 512:(h + 1) * 512], bank_ap[bks[h]], sl["mv"][:, 0:1],
                                                        bc[:, h * 512:(h + 1) * 512], ALU.subtract, ALU.mult),
                       reads=[bank_res[bks[h]], sl["res"], r_bc], writes=[r_x1[t]])
            dve.op(lambda e: e.scalar_tensor_tensor(x1[:, t, :], x1[:, t, :], sl["sd"][:, 0:1], bc[:, 1024:2048], ALU.mult, ALU.add),
                   reads=[r_x1[t], sl["res"], r_bc], writes=[r_x1[t]])
            ln_stats_a(x1[:, t, :], 128, slB[t % 4], r_x1[t])

        def p3_t4(t):
            ln_sqrt(128, slB[t % 4], eps_t)

        def p3_t5(t):
            ln_recip(128, slB[t % 4])

        def p3_t6(t):
            sl = slB[t % 4]
            act.op(lambda e: e.activation(h2n[t % 2], x1[:, t, :], AF.Identity, bias=sl["nm"], scale=sl["sd"][:, 0:1]),
                   reads=[r_x1[t], sl["res"]], writes=[r_h2n[t % 2]])

        def p3_t7(t):
            transpose_evac(h2n[t % 2], 128, r_h2n[t % 2], h2T, r_h2T[t], HALO + t * 128, SC2, SH2, r_mod2)

        p4 = {}

        def p4_setup():
            p4["fT"] = sb(OFF_FT, [128, FC, TB], BF16, "fT")
            p4["r_fT"] = [[newres("fT_%d_%d" % (fc, s), OFF_FT + (fc * TB + s * 512) * 2, 1024) for s in range(NS)] for fc in range(FC)]
            p4["rr"] = [sb(OFF_XT[0] + i * 1024, [128, 512], BF16, "rr") for i in range(3)]
            p4["r_rr"] = [newres("rr%d" % i, OFF_XT[0] + i * 1024, 1024) for i in range(3)]
            p4["rot"] = Rot(range(3))

        def p4_group(wu, ru, g, jj, s):
            fc = 4 * g + jj
            c0 = HALO + s * 512
            bi = brot.next()
            grp = PEGroup(pe, bank_res[bi])
            for kc in range(KC):
                grp.mm(bank_ap[bi], wu[:, kc, jj * 128:(jj + 1) * 128], h2T[:, kc, c0:c0 + 512],
                       [ru] + [r for pr in r_h2T[s * 4:(s + 1) * 4] for r in pr], last=(kc == KC - 1))
            k = p4["rot"].next()
            rr, r_rr = p4["rr"], p4["r_rr"]
            act.op(lambda e: e.activation(rr[k], bank_ap[bi], AF.Relu, bias=pvec[:, PV_BUP + fc:PV_BUP + fc + 1], scale=1.0),
                   reads=[bank_res[bi], r_pvec], writes=[r_rr[k]])
            dve.op(lambda e: e.tensor_tensor(p4["fT"][:, fc, s * 512:(s + 1) * 512], rr[k], rr[k], ALU.mult),
                   reads=[r_rr[k]], writes=[p4["r_fT"][fc][s]])

        NHEAD = 4
        head = []
        stages = [p3_t0, lambda t: (p3_t1(t), p3_t2(t)), lambda t: (p3_t3(t), p3_t4(t)), lambda t: (p3_t5(t), p3_t6(t)), p3_t7]
        nsteps3 = NT + len(stages) - 1
        for step in range(nsteps3):
            for sidx in reversed(range(len(stages))):
                t = step - sidx
                if 0 <= t < NT:
                    stages[sidx](t)
            if step == NT:
                ws_release(iwo0)
                ws_release(iwo1)
                p4_setup()
            if step >= NT and len(head) < NHEAD:
                g = len(head)
                head.append(ws_get())
                for jj in range(4):
                    p4_group(head[g][0], head[g][1], g, jj, 0)
        while len(head) < NHEAD:
            g = len(head)
            head.append(ws_get())
            for jj in range(4):
                p4_group(head[g][0], head[g][1], g, jj, 0)
        for g in range(NHEAD):
            for jj in range(4):
                p4_group(head[g][0], head[g][1], g, jj, 1)
            ws_release(head[g][2])
        for g in range(NHEAD, 8):
            wu, ru, iu = ws_get()
            for jj in range(4):
                for s in range(NS):
                    p4_group(wu, ru, g, jj, s)
            ws_release(iu)
        fT, r_fT = p4["fT"], p4["r_fT"]

        sl5 = mkslots(8, OFF_TQ)

        def p5_resid(h, t):
            dve.op(lambda e: e.tensor_tensor(x1[:, t, h * 512:(h + 1) * 512], bank_ap[t], x1[:, t, h * 512:(h + 1) * 512], ALU.add),
                   reads=[bank_res[t], r_x1[t]], writes=[r_x1[t]])
            ln_stats_a(x1[:, t, :], 128, sl5[t], r_x1[t], halves=(h,), aggr=(h == 1))

        def p5_e1(t):
            ln_sqrt(128, sl5[t], eps2_t)

        def p5_e2(t):
            ln_recip(128, sl5[t], nmr=False)
            ln_affine_dve(x1[:, t, :], 128, sl5[t], r_x1[t], bc[:, 2048:3072], bc[:, 3072:4096])

        def p5_e3(t):
            r0 = b * TB + t * 128
            ev = sp.dma(y[r0:r0 + 128, :], x1[:, t, :], r_x1[t], reads=[r_x1[t]])
            out_events.append(ev)

        p5_stages = [None, p5_e1, p5_e2, p5_e3]

        def p5_tail_step(t):
            for k_ in (3, 2, 1):
                if 0 <= t - k_ < NT:
                    p5_stages[k_](t - k_)

        def mm_piece(grp, t, h, gq, wd, rd, mark_last):
            if gq == 0:
                grp.mm(bank_ap[t], e0, bdrow[:, h * 512:(h + 1) * 512], [r_const, r_rows])
            for kc in range(KC):
                fc = gq * 8 + kc
                lastmm = (gq == 3 and kc == KC - 1)
                mk = [rd] if (mark_last and kc == KC - 1 and not lastmm) else ()
                grp.mm(bank_ap[t], fT[:, fc, t * 128:(t + 1) * 128], wd[:, kc, :],
                       [r_fT[fc][t // 4], rd], last=lastmm, mark=mk)

        grps = [PEGroup(pe, bank_res[t]) for t in range(NT)]
        for gq in range(4):
            wd, rd, idn = ws_get()
            for t in range(NT):
                mm_piece(grps[t], t, 0, gq, wd, rd, mark_last=(t == NT - 1))
                if gq == 3:
                    p5_resid(0, t)
            ws_release(idn)
        ws_flush()
        grps = [PEGroup(pe, bank_res[t]) for t in range(NT)]
        for gq in range(1):
            wd, rd, idn = ws_get()
            for t in range(NT):
                mm_piece(grps[t], t, 1, gq, wd, rd, mark_last=(t == NT - 1))
            ws_release(idn)
        tl = [ws_get() for _ in range(3)]
        nsteps = []
        if b + 1 < NB:
            ctx, nsteps = p1_setup(b + 1, low_banks=True)
            for _ in range(3):
                nsteps.pop(0)()
        for t in range(NT):
            for q_ in range(3):
                mm_piece(grps[t], t, 1, 1 + q_, tl[q_][0], tl[q_][1], mark_last=False)
            p5_resid(1, t)
            p5_tail_step(t)
            if nsteps:
                nsteps.pop(0)()
        for q_ in range(3):
            ws_release(tl[q_][2])
        for t in range(NT, NT + 3):
            p5_tail_step(t)

    for ev in out_events:
        sp.wait_ev(ev)
    if needed is None:
        return {k: sorted(v) for k, v in fw.rec.items()}
    return nc


_CACHE = {}


def kernel(x, c, w_ada, b_ada, w_in, b_in, conv_a_w, conv_a_b, ln_a_g, ln_a_b, w_a_out, b_a_out, conv_b_w, w_b_out,
           w_o, b_o, ln1_g, ln1_b, w_up, b_up, w_down, b_down, ln2_g, ln2_b):
    f = lambda a: np.ascontiguousarray(np.asarray(a, dtype=np.float32))
    x2 = f(x).reshape(SEQ, D)

    def pp(vec):
        vec = f(vec).reshape(-1, 128)
        return vec.T

    pv = np.zeros((128, NPV), np.float32)
    pv[:, PV_BIN:PV_BIN + 56] = pp(b_in)
    pv[:, PV_CAB:PV_CAB + 8] = pp(conv_a_b)
    pv[:, PV_LAG:PV_LAG + 8] = pp(ln_a_g)
    pv[:, PV_LAB:PV_LAB + 8] = pp(ln_a_b)
    pv[:, PV_BAO:PV_BAO + 8] = pp(b_a_out)
    pv[:, PV_BUP:PV_BUP + 32] = pp(b_up)
    caw = f(conv_a_w).reshape(31, 8, 128)
    pv[:, PV_CAW:PV_CAW + 248] = caw.transpose(2, 1, 0).reshape(128, 248)
    cbw = f(conv_b_w).reshape(3, 8, 128)
    pv[:, PV_CBW:PV_CBW + 24] = cbw.transpose(2, 1, 0).reshape(128, 24)
    pv[:, PV_C:PV_C + 8] = pp(c)
    rows = np.concatenate([f(b_ada).reshape(-1), f(b_o).reshape(-1), f(b_down).reshape(-1)])[None, :]
    bcv = np.concatenate([f(ln1_g).reshape(-1), f(ln1_b).reshape(-1), f(ln2_g).reshape(-1), f(ln2_b).reshape(-1)])
    bcm = np.ascontiguousarray(np.broadcast_to(bcv[None, :], (128, 4096)))
    wts = dict(w_ada=f(w_ada).reshape(D, 6 * D), w_in=f(w_in).reshape(D, 7 * D), w_a_out=f(w_a_out).reshape(D, D),
               w_b_out=f(w_b_out).reshape(D, D), w_o=f(w_o).reshape(D, D), w_up=f(w_up).reshape(D, DFF),
               w_down=f(w_down).reshape(DFF, D))
    in_maps = []
    for k in range(NCORES):
        xsk = np.zeros((TPC + HALO, D), np.float32)
        if k == 0:
            xsk[HALO:] = x2[0:TPC]
        else:
            xsk[:] = x2[k * TPC - HALO:(k + 1) * TPC]
        pvk = pv.copy()
        pvk[:, PV_MASK] = 0.0 if k == 0 else 1.0
        pvk[:, PV_MASK + 1] = 1.0
        m = dict(xs=xsk, pvec=pvk, rows=rows, bc=bcm)
        m.update(wts)
        in_maps.append(m)
    if "nc" not in _CACHE:
        rec = build_program(None)
        _CACHE["nc"] = build_program(rec)
    res = run_bass_kernel_spmd(_CACHE["nc"], in_maps, core_ids=list(range(NCORES)))
    out = np.concatenate([np.asarray(r["y"]) for r in res.results], axis=0)
    return out.reshape(1, SEQ, D).astype(np.float32)
```

```python
import bisect
import numpy as np
import concourse.bass as bass
import concourse.mybir as mybir
from concourse.bass_utils import run_bass_kernel_spmd

F32 = mybir.dt.float32
BF16 = mybir.dt.bfloat16
AF = mybir.ActivationFunctionType
ALU = mybir.AluOpType

NCORES = 8
D = 1024
KC = 8
SEQ = 16384
TPC = SEQ // NCORES
NB = 2
TB = TPC // NB
NS = TB // 512
NT = TB // 128
HALO = 32
TW = HALO + TB
DFF = 4096
FC = DFF // 128
ALPHA = float(2.0 ** 0.25)
EPS = 1e-5

PV_BIN, PV_CAB, PV_LAG, PV_LAB, PV_BAO, PV_BUP, PV_CAW, PV_CBW, PV_C, PV_MASK = 0, 56, 64, 72, 80, 88, 120, 368, 392, 400
NPV = 404


class Res:
    __slots__ = ("name", "w", "r", "dsem", "dcnt", "lo", "hi", "dead", "excl")

    def __init__(self, name, lo=None, hi=None):
        self.name = name
        self.w = None
        self.r = {}
        self.dsem = None
        self.dcnt = 0
        self.lo = lo
        self.hi = hi
        self.dead = False
        self.excl = False


class Eng:
    def __init__(self, fw, handle, name, selfsync=True):
        self.fw = fw
        self.h = handle
        self.name = name
        self.sem = fw.nc.alloc_semaphore("s_" + name)
        self.idx = 0
        self.cnt = 0
        self.seen = {}
        self.selfsync = selfsync

    def wait_ev(self, ev, same_ok=False):
        if ev is None:
            return
        sem, val, key = ev
        if key == self.name and (same_ok or not self.selfsync):
            return
        sid = id(sem)
        if self.seen.get(sid, 0) >= val:
            return
        self.h.wait_ge(sem, self.fw.sem_value(key, val))
        self.seen[sid] = val

    def deps(self, reads, writes):
        for r in reads:
            assert not r.dead, ("read of dead res", r.name)
            self.wait_ev(r.w)
            if r.excl:
                for ev in r.r.values():
                    self.wait_ev(ev, same_ok=True)
        for w in writes:
            assert not w.dead, ("write of dead res", w.name)
            self.wait_ev(w.w)
            for ev in w.r.values():
                self.wait_ev(ev)

    def mark(self, inst, reads=(), writes=()):
        self.idx += 1
        if self.fw.needs_inc(self.name, self.idx):
            self.cnt += 1
            inst.then_inc(self.sem, 1)
        ev = (self.sem, self.idx, self.name)
        for r in reads:
            r.r[self.name] = ev
        for w in writes:
            w.w = ev
            w.r = {}
        return ev

    def op(self, fn, reads=(), writes=()):
        self.deps(reads, writes)
        inst = fn(self.h)
        self.mark(inst, reads, writes)
        return inst

    def dma(self, out, in_, owner, reads=(), writes=()):
        self.deps(reads, writes)
        if owner.dsem is None:
            self.fw.nsem += 1
            owner.dsem = self.fw.nc.alloc_semaphore("d%d_%s" % (self.fw.nsem, owner.name))
        owner.dcnt += 16
        inst = self.h.dma_start(out=out, in_=in_)
        inst.then_inc(owner.dsem, 16)
        ev = (owner.dsem, owner.dcnt, "dma_" + owner.name)
        for r in reads:
            r.r["dma_%s_%d" % (owner.name, owner.dcnt)] = ev
        for w in writes:
            w.w = ev
            w.r = {}
        return ev


class PEGroup:
    def __init__(self, pe, bank):
        self.pe = pe
        self.bank = bank
        self.first = True
        self.reads = {}

    def mm(self, out, lhsT, rhs, reads, last=False, mark=()):
        pe = self.pe
        if self.first:
            pe.deps(reads, [self.bank])
        else:
            pe.deps(reads, [])
        inst = pe.h.matmul(out, lhsT, rhs, start=self.first, stop=last)
        self.first = False
        for r in reads:
            self.reads[id(r)] = r
        if last:
            pe.mark(inst, list(self.reads.values()), [self.bank])
            self.reads = {}
        elif mark:
            pe.mark(inst, list(mark), [])
            for r in mark:
                self.reads.pop(id(r), None)
        return inst


class FW:
    def __init__(self, nc, needed=None):
        self.nc = nc
        self.needed = needed
        self.rec = {}
        self.registry = []
        self.nsem = 0
        self.pe = Eng(self, nc.tensor, "pe", selfsync=False)
        self.act = Eng(self, nc.scalar, "act")
        self.dve = Eng(self, nc.vector, "dve")
        self.pool = Eng(self, nc.gpsimd, "pool")
        self.sp = Eng(self, nc.sync, "sp")

    def needs_inc(self, key, idx):
        if self.needed is None:
            return True
        lst = self.needed.get(key, [])
        i = bisect.bisect_left(lst, idx)
        return i < len(lst) and lst[i] == idx

    def sem_value(self, key, val):
        if key.startswith("dma_"):
            return val
        if self.needed is None:
            self.rec.setdefault(key, set()).add(val)
            return val
        lst = self.needed[key]
        i = bisect.bisect_left(lst, val)
        assert i < len(lst) and lst[i] == val, (key, val)
        return i + 1

    def new_res(self, name, lo, hi):
        r = Res(name, lo, hi)
        keep = []
        for o in self.registry:
            if o.lo < hi and lo < o.hi:
                if o.w is not None:
                    k = o.w[2] if not o.w[2].startswith("dma_") else o.w[2] + str(o.w[1])
                    if k not in r.r or r.r[k][1] < o.w[1]:
                        r.r[k] = o.w
                for k, ev in o.r.items():
                    if k not in r.r or r.r[k][1] < ev[1]:
                        r.r[k] = ev
                o.dead = True
                for glo, ghi in ((o.lo, lo), (hi, o.hi)):
                    if glo < ghi:
                        gh = Res(o.name + "~", glo, ghi)
                        gh.w = o.w
                        gh.r = dict(o.r)
                        gh.dead = True
                        keep.append(gh)
            else:
                keep.append(o)
        keep.append(r)
        self.registry = keep
        return r


def new_group(fw, names, lo, hi):
    base = fw.new_res("grp", lo, hi)
    fw.registry.remove(base)
    out = []
    for n in names:
        r = Res(n, lo, hi)
        r.r = dict(base.r)
        out.append(r)
    fw.registry.extend(out)
    return out


class Rot:
    def __init__(self, items):
        self.items = list(items)
        self.i = 0

    def next(self):
        x = self.items[self.i % len(self.items)]
        self.i += 1
        return x


def build_program(needed=None):
    nc = bass.Bass("TRN2", target_bir_lowering=False)
    fw = FW(nc, needed)
    pe, act, dve, pool, sp = fw.pe, fw.act, fw.dve, fw.pool, fw.sp

    def din(name, shape):
        return nc.dram_tensor(name, shape, F32, kind="ExternalInput").ap()

    xs = din("xs", [TPC + HALO, D])
    pvec_d = din("pvec", [128, NPV])
    rows_d = din("rows", [1, 8192])
    bc_d = din("bc", [128, 4096])
    w_ada = din("w_ada", [D, 6 * D])
    w_in = din("w_in", [D, 7 * D])
    w_a_out = din("w_a_out", [D, D])
    w_b_out = din("w_b_out", [D, D])
    w_o = din("w_o", [D, D])
    w_up = din("w_up", [D, DFF])
    w_down = din("w_down", [DFF, D])
    y = nc.dram_tensor("y", [TPC, D], F32, kind="ExternalOutput").ap()

    C_SZ = 38400
    W0 = C_SZ
    NWB = 4
    A0 = W0 + NWB * 8192
    A_SZ = 132096 + 4096
    TOTAL = A0 + A_SZ
    lo, hi = nc.bump_sbuf(TOTAL)
    cnt = [0]

    def sb(off, shape, dtype, name):
        cnt[0] += 1
        return nc.alloc_sbuf_tensor_at("%s_%d" % (name, cnt[0]), list(shape), dtype, offset=lo + off).ap()

    def newres(name, off, nbytes):
        return fw.new_res(name, off, off + nbytes)

    o = 0
    pvec = sb(o, [128, NPV], F32, "pvec"); o += 1664
    eps_t = sb(o, [128, 1], F32, "eps"); eps2_t = sb(o + 32, [128, 1], F32, "eps2"); o += 64
    ident = sb(o, [128, 128], BF16, "ident"); o += 256
    ones_b = sb(o, [128, 128], BF16, "onesb"); o += 256
    ones_f = sb(o, [1, 128], F32, "onesf"); o += 512
    dg3 = sb(o, [128, 24, 128], BF16, "dg3"); o += 6144
    borow = sb(o, [128, D], BF16, "borow"); o += 2048
    bdrow = sb(o, [128, D], BF16, "bdrow"); o += 2048
    g1b = sb(o, [128, D], F32, "g1b"); o += 4096
    g2b = sb(o, [128, D], F32, "g2b"); o += 4096
    bc = sb(o, [128, 4096], F32, "bc"); o += 16384
    modT = sb(o, [128, 32], F32, "modT"); o += 128
    cact = sb(o, [128, 8], BF16, "cact"); o += 64
    NST = 4
    st_sb, mv_sb, sd_sb, nm_sb, st_res = [], [], [], [], []
    for i in range(NST):
        st_sb.append(sb(o, [128, 2, 6], F32, "st")); o += 64
        mv_sb.append(sb(o, [128, 2], F32, "mv")); o += 32
        sd_sb.append(sb(o, [128, 1], F32, "sd")); o += 32
        nm_sb.append(sb(o, [128, 1], F32, "nm")); o += 32
        st_res.append(Res("st%d" % i))
    assert o <= C_SZ, o
    st_rot = Rot(range(NST))
    r_pvec, r_const, r_bc, r_rows, r_mod, r_dg3 = Res("pvec"), Res("const"), Res("bc"), Res("rows"), Res("mod"), Res("dg3")
    r_cact, r_ident = Res("cact"), Res("ident")

    wb_ap = [sb(W0 + i * 8192, [128, 8, 512], BF16, "wb") for i in range(NWB)]
    wb_res = [Res("wb%d" % i) for i in range(NWB)]

    bank_ap = [nc.alloc_psum_tensor("bank%d" % i, [128, 512], F32).ap() for i in range(8)]
    bank_res = [Res("bank%d" % i) for i in range(8)]
    for r_ in bank_res:
        r_.excl = True

    OFF_HT = A0
    OFF_XT = [A0 + 16896 + i * 4096 for i in range(3)]
    OFF_XN = [A0 + 29184 + i * 2048 for i in range(2)]
    OFF_TMP = A0 + 16896
    OFF_X1 = A0 + 33280
    OFF_FT = A0 + 66048
    OFF_U = A0 + 33280
    OFF_CB = A0 + 50176
    OFF_U2 = A0 + 82944
    OFF_V = A0 + 99328
    OFF_MG = A0 + 115712
    OFF_TQ = A0 + 132096

    def wsrc(w, r0, c0):
        return w.rearrange("(kc p) c -> p kc c", p=128)[:, r0 // 128:r0 // 128 + 8, c0:c0 + 512]

    wtiles = []
    wscale = {}
    for nt in range(4):
        wtiles.append(wsrc(w_ada, 0, nt * 512))
    for b in range(NB):
        for g in range(2):
            wtiles.append(wsrc(w_in, 0, 1024 + g * 512))
            wtiles.append(wsrc(w_in, 0, g * 512))
        if b == 0:
            for nt in range(4, 12):
                wtiles.append(wsrc(w_ada, 0, nt * 512))
        for g in range(2):
            wtiles.append(wsrc(w_in, 0, 4096 + g * 512))
            wtiles.append(wsrc(w_in, 0, 3072 + g * 512))
        for g in range(2):
            wtiles.append(wsrc(w_in, 0, 2048 + g * 512))
        for g in range(2):
            wtiles.append(wsrc(w_in, 0, 5120 + g * 512))
            wtiles.append(wsrc(w_a_out, 0, g * 512))
        for g in range(2):
            wtiles.append(wsrc(w_in, 0, 6144 + g * 512))
            wtiles.append(wsrc(w_b_out, 0, g * 512))
        for h in range(2):
            wscale[len(wtiles)] = ("g1", h)
            wtiles.append(wsrc(w_o, 0, h * 512))
        for g in range(8):
            wtiles.append(wsrc(w_up, 0, g * 512))
        for h in range(2):
            for gq in range(4):
                wscale[len(wtiles)] = ("g2", h)
                wtiles.append(wsrc(w_down, gq * 1024, h * 512))

    ws = {"issued": 0, "released": -1, "next": 0, "pending": None}

    def ws_prescale(i):
        which, h = wscale[i]
        gb = g1b if which == "g1" else g2b
        wt, wr = wb_ap[i % NWB], wb_res[i % NWB]
        pool.op(lambda e: e.tensor_tensor(wt, wt, gb[:, h * 512:(h + 1) * 512].unsqueeze(1).broadcast_to([128, 8, 512]), ALU.mult),
                reads=[wr, r_mod2], writes=[wr])

    def ws_flush():
        if ws["pending"] is not None:
            ws_prescale(ws["pending"])
            ws["pending"] = None

    def ws_pump():
        while ws["issued"] < len(wtiles) and ws["issued"] <= ws["released"] + NWB:
            i = ws["issued"]
            s = i % NWB
            pool.dma(wb_ap[s], wtiles[i], wb_res[s], writes=[wb_res[s]])
            ws["issued"] += 1
            ws_flush()
            if i in wscale:
                ws["pending"] = i

    def ws_get():
        i = ws["next"]
        ws["next"] += 1
        assert i < ws["issued"], "weight tile not yet issued"
        if ws["pending"] == i:
            ws_flush()
        return wb_ap[i % NWB], wb_res[i % NWB], i

    def ws_release(i):
        assert i == ws["released"] + 1, (i, ws["released"])
        ws["released"] = i
        ws_pump()

    sp.dma(pvec, pvec_d, r_pvec, writes=[r_pvec])
    ws_pump()
    mod_row = sb(OFF_V, [1, 6144], F32, "modrow")
    identf = sb(OFF_U2, [128, 128], F32, "identf")
    r_modrow = newres("modrow", OFF_V, 24576)
    r_identf = newres("identf", OFF_U2, 512)
    sp.dma(mod_row, rows_d[0:1, 0:6144], r_modrow, writes=[r_modrow])
    sp.dma(bc, bc_d, r_bc, writes=[r_bc])
    rowstage = sb(OFF_V + 24576, [1, 2048], F32, "rowstage")
    r_rowstage = newres("rowstage", OFF_V + 24576, 8192)
    sp.dma(rowstage, rows_d[0:1, 6144:8192], r_rowstage, writes=[r_rowstage])
    dve.op(lambda e: e.memset(eps_t, EPS), writes=[r_const])
    dve.op(lambda e: e.memset(eps2_t, EPS / (ALPHA * ALPHA)), writes=[r_const])
    dve.op(lambda e: e.memset(ones_b, 1.0), writes=[r_const])
    dve.op(lambda e: e.memset(ones_f, 1.0), writes=[r_const])
    pool.op(lambda e: e.memset(identf, 1.0), writes=[r_identf])
    pool.op(lambda e: e.affine_select(out=identf, in_=identf, pattern=[[-1, 128]], compare_op=ALU.is_equal,
                                      fill=0.0, base=0, channel_multiplier=1), reads=[r_identf], writes=[r_identf])
    dve.op(lambda e: e.tensor_copy(ident, identf), reads=[r_identf], writes=[r_ident])
    e0 = sb(OFF_TQ + 3584, [128, 128], BF16, "e0")
    dve.op(lambda e: e.memset(e0, 0.0), writes=[r_const])
    dve.op(lambda e: e.memset(e0[0:1, :], 1.0), writes=[r_const])
    dve.op(lambda e: e.memset(borow, 0.0), writes=[r_rows])
    dve.op(lambda e: e.memset(bdrow, 0.0), writes=[r_rows])
    zrow = sb(OFF_TQ + 2560, [1, 512], BF16, "zrow")
    dve.op(lambda e: e.memset(zrow, 0.0), writes=[r_const])
    ident32 = sb(OFF_TQ + 2048, [128, 128], F32, "ident32")
    r_ident32 = Res("ident32")
    dve.op(lambda e: e.tensor_copy(ident32, identf), reads=[r_identf], writes=[r_ident32])
    pool.op(lambda e: e.tensor_tensor(dg3, ident.unsqueeze(1).broadcast_to([128, 24, 128]),
                                      pvec[:, PV_CBW:PV_CBW + 24].unsqueeze(2).broadcast_to([128, 24, 128]), ALU.mult),
            reads=[r_ident, r_pvec], writes=[r_dg3])
    act.op(lambda e: e.activation(cact, pvec[:, PV_C:PV_C + 8], AF.Silu), reads=[r_pvec], writes=[r_cact])

    brot = Rot(range(6))
    SH1, SC1, SH2, SC2 = 0, 8, 16, 24
    r_mod1, r_mod2 = Res("mod1"), Res("mod2")

    def ada_tile(nt):
        wt, wr, wi = ws_get()
        bi = brot.next()
        g = PEGroup(pe, bank_res[bi])
        for kc in range(KC):
            g.mm(bank_ap[bi][0:1, :], cact[:, kc:kc + 1], wt[:, kc, :], [r_cact, wr], last=(kc == KC - 1))
        ws_release(wi)
        dve.op(lambda e: e.tensor_tensor(mod_row[0:1, nt * 512:(nt + 1) * 512], bank_ap[bi][0:1, :],
                                         mod_row[0:1, nt * 512:(nt + 1) * 512], ALU.add),
               reads=[bank_res[bi], r_modrow], writes=[r_modrow])

    def mod_cols(pairs, rres):
        bi = brot.next()
        for cb, sec, _ in pairs:
            for j in range(8):
                g = PEGroup(pe, bank_res[bi])
                g.mm(bank_ap[bi][:, cb + j:cb + j + 1], mod_row[0:1, sec * 1024 + j * 128: sec * 1024 + (j + 1) * 128],
                     ones_f[0:1, 0:1], [r_modrow, r_const])
                g.mm(bank_ap[bi][:, cb + j:cb + j + 1], ones_b[0:1, :], zrow[0:1, 0:1], [r_const], last=True)
        for cb, sec, one in pairs:
            if one:
                dve.op(lambda e: e.tensor_scalar(modT[:, cb:cb + 8], bank_ap[bi][:, cb:cb + 8], 1.0, None, ALU.add),
                       reads=[bank_res[bi]], writes=[rres])
            else:
                dve.op(lambda e: e.tensor_copy(modT[:, cb:cb + 8], bank_ap[bi][:, cb:cb + 8]), reads=[bank_res[bi]], writes=[rres])

    def mod_gates():
        for gb, sec in ((g1b, 2), (g2b, 5)):
            for h in range(2):
                bi = brot.next()
                g = PEGroup(pe, bank_res[bi])
                g.mm(bank_ap[bi], ones_f[0:1, 0:128], mod_row[0:1, sec * 1024 + h * 512: sec * 1024 + (h + 1) * 512],
                     [r_modrow, r_const])
                g.mm(bank_ap[bi], ones_b[0:1, :], zrow[0:1, :], [r_const], last=True)
                dve.op(lambda e: e.tensor_scalar(gb[:, h * 512:(h + 1) * 512], bank_ap[bi], 1.0, 1.0 / ALPHA, ALU.add, ALU.mult),
                       reads=[bank_res[bi]], writes=[r_mod2])

    def mkslots(n, base_off=None):
        out = []
        for i in range(n):
            off = base_off + i * 160
            out.append(dict(st=sb(off, [128, 2, 6], F32, "st"), mv=sb(off + 64, [128, 2], F32, "mv"),
                            sd=sb(off + 96, [128, 1], F32, "sd"), nm=sb(off + 128, [128, 1], F32, "nm"),
                            res=newres("stslot", off, 160)))
        return out

    cslots = [dict(st=st_sb[i], mv=mv_sb[i], sd=sd_sb[i], nm=nm_sb[i], res=st_res[i]) for i in range(NST)]

    def ln_stats_a(src, npart, sl, rsrc, halves=(0, 1), aggr=True):
        st, mv, res = sl["st"], sl["mv"], sl["res"]
        for hh in halves:
            dve.op(lambda e: e.bn_stats(st[:npart, hh, :], src[:npart, hh * 512:(hh + 1) * 512]), reads=[rsrc, res], writes=[res])
        if aggr:
            dve.op(lambda e: e.bn_aggr(mv[:npart, :], st[:npart, :, :]), reads=[res], writes=[res])

    def ln_sqrt(npart, sl, eps_ap):
        act.op(lambda e: e.activation(sl["sd"][:npart, :], sl["mv"][:npart, 1:2], AF.Sqrt, bias=eps_ap[:npart, :], scale=1.0),
               reads=[sl["res"], r_const], writes=[sl["res"]])

    def ln_recip(npart, sl, nmr=True):
        dve.op(lambda e: e.reciprocal(sl["sd"][:npart, :], sl["sd"][:npart, :]), reads=[sl["res"]], writes=[sl["res"]])
        if nmr:
            dve.op(lambda e: e.tensor_scalar(sl["nm"][:npart, :], sl["mv"][:npart, 0:1], sl["sd"][:npart, 0:1], -1.0, ALU.mult, ALU.mult),
                   reads=[sl["res"]], writes=[sl["res"]])

    def ln_affine_dve(dst, npart, sl, rdst, g_b, b_b):
        dve.op(lambda e: e.scalar_tensor_tensor(dst, dst, sl["mv"][:npart, 0:1], g_b, ALU.subtract, ALU.mult),
               reads=[rdst, sl["res"], r_bc], writes=[rdst])
        dve.op(lambda e: e.scalar_tensor_tensor(dst, dst, sl["sd"][:npart, 0:1], b_b, ALU.mult, ALU.add),
               reads=[rdst, sl["res"], r_bc], writes=[rdst])

    pst_rot = Rot((6, 7))

    def transpose_evac(src_bf, npart, rsrc, dst, rdst, col0, sc_col, sh_col, rmod, n_dve=0, bank=None):
        if bank is not None:
            ba = bb = bank
            n_dve = 0
        elif n_dve == 0:
            ba = bb = pst_rot.next()
        else:
            ba, bb = 6, 7
        psa = bank_ap[ba].bitcast(BF16).rearrange("p (k t) -> p k t", k=8)
        psb = bank_ap[bb].bitcast(BF16).rearrange("p (k t) -> p k t", k=8)
        wr_banks = [bank_res[ba]] if ba == bb else [bank_res[ba], bank_res[bb]]
        pe.deps([rsrc, r_ident], wr_banks)
        inst = None
        for kc in range(KC):
            ps = psb if kc >= KC - n_dve else psa
            inst = pe.h.transpose(ps[:, kc, 0:npart], src_bf[:npart, kc * 128:(kc + 1) * 128], ident[:npart, :npart])
        pe.mark(inst, [rsrc, r_ident], wr_banks)
        for kc in range(KC):
            if sc_col is None:
                if kc >= KC - n_dve:
                    dve.op(lambda e: e.tensor_copy(dst[:, kc, col0:col0 + npart], psb[:, kc, 0:npart]),
                           reads=[bank_res[bb]], writes=[rdst[1]])
                else:
                    act.op(lambda e: e.activation(dst[:, kc, col0:col0 + npart], psa[:, kc, 0:npart], AF.Identity),
                           reads=[bank_res[ba]], writes=[rdst[0]])
            elif kc >= KC - n_dve:
                dve.op(lambda e: e.tensor_scalar(dst[:, kc, col0:col0 + npart], psb[:, kc, 0:npart],
                                                 modT[:, sc_col + kc:sc_col + kc + 1], modT[:, sh_col + kc:sh_col + kc + 1],
                                                 ALU.mult, ALU.add),
                       reads=[bank_res[bb], rmod], writes=[rdst[1]])
            else:
                act.op(lambda e: e.activation(dst[:, kc, col0:col0 + npart], psa[:, kc, 0:npart], AF.Identity,
                                              bias=modT[:, sh_col + kc:sh_col + kc + 1], scale=modT[:, sc_col + kc:sc_col + kc + 1]),
                       reads=[bank_res[ba], rmod], writes=[rdst[0]])

    out_events = []

    def p1_setup(b, low_banks=False, raw=False):
        row0 = b * TB
        hT = sb(OFF_HT, [128, KC, TW], BF16, "hT")
        _g = new_group(fw, ["hT%d%s" % (t, ab) for t in range(NT + 1) for ab in "ab"], OFF_HT, OFF_HT + 16896)
        r_hT = [(_g[2 * t], _g[2 * t + 1]) for t in range(NT + 1)]
        xt = [sb(OFF_XT[i], [128, D], F32, "xt") for i in range(3)]
        r_xt = [newres("xt%d" % i, OFF_XT[i], 4096) for i in range(3)]
        xn = [sb(OFF_XN[i], [128, D], BF16, "xn") for i in range(2)]
        r_xn = [newres("xn%d" % i, OFF_XN[i], 2048) for i in range(2)]
        tiles = [("h", HALO, row0, 0)] + [(t, 128, row0 + HALO + t * 128, HALO + t * 128) for t in range(NT)]
        nti = len(tiles)
        sis = {}

        def p1_m0(i):
            _, npart, r0, _ = tiles[i]
            sp.dma(xt[i % 3][:npart, :], xs[r0:r0 + npart, :], r_xt[i % 3], writes=[r_xt[i % 3]])

        def p1_a(i):
            _, npart, _, _ = tiles[i]
            sis[i] = cslots[st_rot.next()]
            ln_stats_a(xt[i % 3], npart, sis[i], r_xt[i % 3])

        def p1_b(i):
            _, npart, _, _ = tiles[i]
            ln_sqrt(npart, sis[i], eps_t)

        def p1_c(i):
            _, npart, _, _ = tiles[i]
            sl = sis[i]
            if low_banks:
                ln_recip(npart, sl, nmr=True)
                act.op(lambda e: e.activation(xn[i % 2][:npart, :], xt[i % 3][:npart, :], AF.Identity,
                                              bias=sl["nm"][:npart, :], scale=sl["sd"][:npart, 0:1]),
                       reads=[r_xt[i % 3], sl["res"]], writes=[r_xn[i % 2]])
            else:
                ln_recip(npart, sl, nmr=False)
                dve.op(lambda e: e.tensor_scalar(xn[i % 2][:npart, :], xt[i % 3][:npart, :], sl["mv"][:npart, 0:1], sl["sd"][:npart, 0:1],
                                                 ALU.subtract, ALU.mult),
                       reads=[r_xt[i % 3], sl["res"]], writes=[r_xn[i % 2]])

        def p1_m2(i):
            _, npart, _, c0 = tiles[i]
            if low_banks:
                transpose_evac(xn[i % 2], npart, r_xn[i % 2], hT, r_hT[i], c0, SC1, SH1, r_mod1, bank=i % 2)
            elif raw:
                transpose_evac(xn[i % 2], npart, r_xn[i % 2], hT, r_hT[i], c0, None, None, None, n_dve=3)
            else:
                transpose_evac(xn[i % 2], npart, r_xn[i % 2], hT, r_hT[i], c0, SC1, SH1, r_mod1, n_dve=3)

        stages = [p1_m0, lambda i: (p1_a(i), p1_b(i)), p1_c, p1_m2]
        steps = []
        for step in range(nti + len(stages) - 1):
            def run(step=step):
                for sidx in reversed(range(len(stages))):
                    i = step - sidx
                    if 0 <= i < nti:
                        stages[sidx](i)
            steps.append(run)
        return dict(hT=hT, r_hT=r_hT, tiles=tiles), steps

    ctx, steps = p1_setup(0, raw=True)
    nada = 0
    for k_, st_ in enumerate(steps):
        st_()
        if k_ >= 3 and nada < 4:
            ada_tile(nada)
            nada += 1
    while nada < 4:
        ada_tile(nada)
        nada += 1
    mod_cols([(SH1, 0, False), (SC1, 1, True)], r_mod1)
    for i_, (_, npart_, _, c0_) in enumerate(ctx["tiles"]):
        ra_, rb_ = ctx["r_hT"][i_]
        for kc in range(KC):
            view = ctx["hT"][:, kc, c0_:c0_ + npart_]
            if kc < 5:
                dve.op(lambda e: e.tensor_scalar(view, view, modT[:, SC1 + kc:SC1 + kc + 1], modT[:, SH1 + kc:SH1 + kc + 1],
                                                 ALU.mult, ALU.add),
                       reads=[ra_, rb_, r_mod1], writes=[ra_])
            else:
                act.op(lambda e: e.activation(view, view, AF.Identity, bias=modT[:, SH1 + kc:SH1 + kc + 1],
                                              scale=modT[:, SC1 + kc:SC1 + kc + 1]),
                       reads=[ra_, rb_, r_mod1], writes=[rb_])

    nsteps = []
    for b in range(NB):
        row0 = b * TB
        hT, r_hT = ctx["hT"], ctx["r_hT"]

        def hT_reads(seg):
            tl_ = [r_hT[0]] if seg == 0 else r_hT[1 + (seg - 1) * 4: 1 + seg * 4]
            return [r for pr in tl_ for r in pr]

        segs = [(0, HALO), (HALO, 512), (HALO + 512, 512)]

        u = sb(OFF_U, [128, KC, TW], BF16, "u")
        r_u = [[newres("u%d_%d" % (j, s), OFF_U + (j * TW + segs[s][0]) * 2, segs[s][1] * 2) for s in range(3)] for j in range(8)]
        sg = [sb(OFF_TMP + i * 2048, [128, 512], F32, "sg") for i in range(2)]
        r_sg = [newres("sg%d" % i, OFF_TMP + i * 2048, 2048) for i in range(2)]
        sg_rot = Rot(range(2))
        brot = Rot(range(6))
        mask_col = pvec[:, PV_MASK + b:PV_MASK + b + 1]
        dg = [sb(OFF_U2 + i * 7936, [128, 31, 128], BF16, "dg31") for i in range(2)]
        r_dg = [newres("dg31_%d" % i, OFF_U2 + i * 7936, 7936) for i in range(2)]

        def build_dg(j_):
            dve.op(lambda e: e.tensor_tensor(dg[j_ % 2], ident.unsqueeze(1).broadcast_to([128, 31, 128]),
                                             pvec[:, PV_CAW + j_ * 31:PV_CAW + (j_ + 1) * 31].unsqueeze(2).broadcast_to([128, 31, 128]),
                                             ALU.mult),
                   reads=[r_ident, r_pvec], writes=[r_dg[j_ % 2]])

        gw = [ws_get() for _ in range(4)]

        def glu(j, s):
            g, jj = divmod(j, 4)
            (wg, rg, _), (wv, rv, _) = gw[2 * g], gw[2 * g + 1]
            c0, n = segs[s]
            bi = brot.next()
            grp = PEGroup(pe, bank_res[bi])
            for kc in range(KC):
                grp.mm(bank_ap[bi][:, 0:n], wg[:, kc, jj * 128:(jj + 1) * 128], hT[:, kc, c0:c0 + n],
                       [rg] + hT_reads(s), last=(kc == KC - 1))
            k = sg_rot.next()
            act.op(lambda e: e.activation(sg[k][:, 0:n], bank_ap[bi][:, 0:n], AF.Sigmoid,
                                          bias=pvec[:, PV_BIN + 8 + j:PV_BIN + 9 + j], scale=1.0),
                   reads=[bank_res[bi], r_pvec], writes=[r_sg[k]])
            if s == 0:
                dve.op(lambda e: e.tensor_scalar(sg[k][:, 0:n], sg[k][:, 0:n], mask_col, None, ALU.mult),
                       reads=[r_sg[k], r_pvec], writes=[r_sg[k]])
            bi2 = brot.next()
            grp = PEGroup(pe, bank_res[bi2])
            for kc in range(KC):
                grp.mm(bank_ap[bi2][:, 0:n], wv[:, kc, jj * 128:(jj + 1) * 128], hT[:, kc, c0:c0 + n],
                       [rv] + hT_reads(s), last=(kc == KC - 1))
            dve.op(lambda e: e.scalar_tensor_tensor(u[:, j, c0:c0 + n], bank_ap[bi2][:, 0:n],
                                                    pvec[:, PV_BIN + j:PV_BIN + j + 1], sg[k][:, 0:n],
                                                    ALU.add, ALU.mult),
                   reads=[bank_res[bi2], r_pvec, r_sg[k]], writes=[r_u[j][s]])

        for j in range(8):
            if j in (0, 4):
                build_dg(j // 4)
            glu(j, 0)
            glu(j, 1)
            if nsteps:
                nsteps.pop(0)()
        while nsteps:
            nsteps.pop(0)()
        for j in range(8):
            glu(j, 2)
        for q_ in range(4):
            ws_release(gw[q_][2])

        cbuf = sb(OFF_CB, [128, KC, TB], F32, "cbuf")
        r_cb = [[newres("cb%d_%d" % (j, s), OFF_CB + (j * TB + s * 512) * 4, 2048) for s in range(NS)] for j in range(8)]
        cbf = [sb(OFF_TMP + 4096 + i * 1024, [128, 512], BF16, "cbf") for i in range(2)]
        r_cbf = [newres("cbf%d" % i, OFF_TMP + 4096 + i * 1024, 1024) for i in range(2)]
        csq = [sb(OFF_TMP + 6144 + i * 1024, [128, 512], BF16, "csq") for i in range(2)]
        r_csq = [newres("csq%d" % i, OFF_TMP + 6144 + i * 1024, 1024) for i in range(2)]
        brot = Rot(range(4))
        tmp_rot = Rot(range(2))
        sgrp1 = [PEGroup(pe, bank_res[4 + s]) for s in range(NS)]
        sgrp2 = [PEGroup(pe, bank_res[6 + s]) for s in range(NS)]
        pend = []

        def flush_stats():
            while pend:
                s_, t_, j_ = pend.pop(0)
                sgrp1[s_].mm(bank_ap[4 + s_], ones_b, cbf[t_], [r_const, r_cbf[t_]], last=(j_ == 7), mark=[r_cbf[t_]])
                sgrp2[s_].mm(bank_ap[6 + s_], ones_b, csq[t_], [r_const, r_csq[t_]], last=(j_ == 7), mark=[r_csq[t_]])

        for j in range(8):
            d = j % 2
            for s in range(NS):
                bi = brot.next()
                grp = PEGroup(pe, bank_res[bi])
                base = 2 + 512 * s
                for k in range(31):
                    grp.mm(bank_ap[bi], dg[d][:, k, :], u[:, j, base + k: base + k + 512],
                           [r_dg[d], r_u[j][s], r_u[j][s + 1]], last=(k == 30))
                if s == NS - 1 and j + 2 < 8:
                    build_dg(j + 2)
                flush_stats()
                cab = pvec[:, PV_CAB + j:PV_CAB + j + 1]
                act.op(lambda e: e.activation(cbuf[:, j, s * 512:(s + 1) * 512], bank_ap[bi], AF.Identity, bias=cab, scale=1.0),
                       reads=[bank_res[bi], r_pvec], writes=[r_cb[j][s]])
                t = tmp_rot.next()
                act.op(lambda e: e.activation(cbf[t], bank_ap[bi], AF.Identity, bias=cab, scale=1.0),
                       reads=[bank_res[bi], r_pvec], writes=[r_cbf[t]])
                act.op(lambda e: e.activation(csq[t], bank_ap[bi], AF.Square, bias=cab, scale=1.0),
                       reads=[bank_res[bi], r_pvec], writes=[r_csq[t]])
                pend.append((s, t, j))
            if b == 0:
                ada_tile(4 + j)
        flush_stats()
        if b == 0:
            mod_cols([(SH2, 3, False), (SC2, 4, True)], r_mod2)
            mod_gates()
            pool.op(lambda e: e.tensor_tensor(borow[0:1, :], rowstage[0:1, 0:1024], g1b[0:1, :], ALU.mult), reads=[r_rowstage, r_mod2, r_rows], writes=[r_rows])
            pool.op(lambda e: e.tensor_tensor(bdrow[0:1, :], rowstage[0:1, 1024:2048], g2b[0:1, :], ALU.mult), reads=[r_rowstage, r_mod2, r_rows], writes=[r_rows])

        u2 = sb(OFF_U2, [128, KC, TB], BF16, "u2")
        r_u2 = [[newres("u2_%d_%d" % (j, s), OFF_U2 + (j * TB + s * 512) * 2, 1024) for s in range(NS)] for j in range(8)]
        mean = [sb(OFF_MG + 4096 + s * 6144, [128, 512], F32, "mean") for s in range(NS)]
        var = [sb(OFF_MG + 4096 + s * 6144 + 2048, [128, 512], F32, "var") for s in range(NS)]
        rstd = [sb(OFF_MG + 4096 + s * 6144 + 4096, [128, 512], F32, "rstd") for s in range(NS)]
        r_lna = [newres("lna%d" % s, OFF_MG + 4096 + s * 6144, 6144) for s in range(NS)]
        t12 = [sb(OFF_MG + i * 2048, [128, 512], F32, "t12") for i in range(2)]
        r_t12 = [newres("t12_%d" % i, OFF_MG + i * 2048, 2048) for i in range(2)]
        t12_rot = Rot(range(2))
        def p2b_stats():
            for s in range(NS):
                dve.op(lambda e: e.tensor_scalar(mean[s], bank_ap[4 + s], 1.0 / D, None, ALU.mult),
                       reads=[bank_res[4 + s]], writes=[r_lna[s]])
                dve.op(lambda e: e.tensor_tensor(var[s], mean[s], mean[s], ALU.mult), reads=[r_lna[s]], writes=[r_lna[s]])
                dve.op(lambda e: e.scalar_tensor_tensor(var[s], bank_ap[6 + s], 1.0 / D, var[s], ALU.mult, ALU.subtract),
                       reads=[bank_res[6 + s], r_lna[s]], writes=[r_lna[s]])
                act.op(lambda e: e.activation(rstd[s], var[s], AF.Sqrt, bias=eps_t, scale=1.0),
                       reads=[r_lna[s], r_const], writes=[r_lna[s]])
                dve.op(lambda e: e.reciprocal(rstd[s], rstd[s]), reads=[r_lna[s]], writes=[r_lna[s]])

        p = u
        r_p = r_u
        bxt = sg
        r_bxt = r_sg
        brot = Rot(range(6))
        wtl = {}

        def p2c1(j):
            g, jj = divmod(j, 4)
            if jj == 0:
                wtl["x"] = ws_get()
                wtl["c"] = ws_get()
            wx, rx, _ = wtl["x"]
            wc, rc, _ = wtl["c"]
            for s, (c0, n) in enumerate(segs):
                bi = brot.next()
                grp = PEGroup(pe, bank_res[bi])
                for kc in range(KC):
                    grp.mm(bank_ap[bi][:, 0:n], wx[:, kc, jj * 128:(jj + 1) * 128], hT[:, kc, c0:c0 + n],
                           [rx] + hT_reads(s), last=(kc == KC - 1))
                k = sg_rot.next()
                act.op(lambda e: e.activation(bxt[k][:, 0:n], bank_ap[bi][:, 0:n], AF.Identity,
                                              bias=pvec[:, PV_BIN + 32 + j:PV_BIN + 33 + j], scale=1.0),
                       reads=[bank_res[bi], r_pvec], writes=[r_bxt[k]])
                if s == 0:
                    dve.op(lambda e: e.tensor_scalar(bxt[k][:, 0:n], bxt[k][:, 0:n], mask_col, None, ALU.mult),
                           reads=[r_bxt[k], r_pvec], writes=[r_bxt[k]])
                bi2 = brot.next()
                grp = PEGroup(pe, bank_res[bi2])
                for kc in range(KC):
                    grp.mm(bank_ap[bi2][:, 0:n], wc[:, kc, jj * 128:(jj + 1) * 128], hT[:, kc, c0:c0 + n],
                           [rc] + hT_reads(s), last=(kc == KC - 1))
                dve.op(lambda e: e.scalar_tensor_tensor(p[:, j, c0:c0 + n], bank_ap[bi2][:, 0:n],
                                                        pvec[:, PV_BIN + 24 + j:PV_BIN + 25 + j], bxt[k][:, 0:n],
                                                        ALU.add, ALU.mult),
                       reads=[bank_res[bi2], r_pvec, r_bxt[k]], writes=[r_p[j][s]])
            if jj == 3:
                ws_release(wtl["x"][2])
                ws_release(wtl["c"][2])

        def p2b_norm(s, j):
            a = t12_rot.next()
            dve.op(lambda e: e.tensor_tensor(t12[a], cbuf[:, j, s * 512:(s + 1) * 512], mean[s], ALU.subtract),
                   reads=[r_cb[j][s], r_lna[s]], writes=[r_t12[a]])
            dve.op(lambda e: e.tensor_tensor(t12[a], t12[a], rstd[s], ALU.mult),
                   reads=[r_t12[a], r_lna[s]], writes=[r_t12[a]])
            act.op(lambda e: e.activation(u2[:, j, s * 512:(s + 1) * 512], t12[a], AF.Silu,
                                          bias=pvec[:, PV_LAB + j:PV_LAB + j + 1], scale=pvec[:, PV_LAG + j:PV_LAG + j + 1]),
                   reads=[r_t12[a], r_pvec], writes=[r_u2[j][s]])

        brot = Rot(range(4))
        p2c1(0)
        p2c1(1)
        p2b_stats()
        brot = Rot(range(6))
        for j in range(2, 8):
            p2c1(j)
            for s in range(NS):
                p2b_norm(s, j - 2)
        for j in range(6, 8):
            for s in range(NS):
                p2b_norm(s, j)

        v = sb(OFF_V, [128, KC, TB], BF16, "v")
        r_v = [[newres("v_%d_%d" % (j, s), OFF_V + (j * TB + s * 512) * 2, 1024) for s in range(NS)] for j in range(8)]
        for g in range(2):
            wgb, rgb, igb = ws_get()
            for jj in range(4):
                j = 4 * g + jj
                for s in range(NS):
                    c0 = HALO + s * 512
                    bi = brot.next()
                    grp = PEGroup(pe, bank_res[bi])
                    for kc in range(KC):
                        grp.mm(bank_ap[bi], wgb[:, kc, jj * 128:(jj + 1) * 128], hT[:, kc, c0:c0 + 512],
                               [rgb] + hT_reads(s + 1), last=(kc == KC - 1))
                    k = sg_rot.next()
                    act.op(lambda e: e.activation(bxt[k], bank_ap[bi], AF.Identity,
                                                  bias=pvec[:, PV_BIN + 16 + j:PV_BIN + 17 + j], scale=1.0),
                           reads=[bank_res[bi], r_pvec], writes=[r_bxt[k]])
                    bi2 = brot.next()
                    grp = PEGroup(pe, bank_res[bi2])
                    for k3 in range(3):
                        grp.mm(bank_ap[bi2], dg3[:, j * 3 + k3, :], p[:, j, c0 - 2 + k3: c0 - 2 + k3 + 512],
                               [r_dg3, r_p[j][s], r_p[j][s + 1]], last=(k3 == 2))
                    dve.op(lambda e: e.tensor_tensor(v[:, j, s * 512:(s + 1) * 512], bank_ap[bi2], bxt[k], ALU.mult),
                           reads=[bank_res[bi2], r_bxt[k]], writes=[r_v[j][s]])
            ws_release(igb)

        m1 = sb(OFF_CB, [128, KC, TB], F32, "m1")
        r_m1 = r_cb
        mg = sb(OFF_MG, [128, KC, TB], BF16, "merged")
        r_mg = [[newres("mg_%d_%d" % (j, s), OFF_MG + (j * TB + s * 512) * 2, 1024) for s in range(NS)] for j in range(8)]
        tb_ = [sb(OFF_TMP + 4096 + i * 2048, [128, 512], F32, "tb") for i in range(2)]
        r_tb = [newres("tb%d" % i, OFF_TMP + 4096 + i * 2048, 2048) for i in range(2)]
        tb_rot = Rot(range(2))
        for pss in range(2):
            for g in range(2):
                wga, rga, iga = ws_get()
                wo_, ro_, io_ = ws_get()
                for jj in range(4):
                    j = 4 * g + jj
                    for s in range(NS):
                        c0 = HALO + s * 512
                        bi = brot.next()
                        grp = PEGroup(pe, bank_res[bi])
                        for kc in range(KC):
                            grp.mm(bank_ap[bi], wga[:, kc, jj * 128:(jj + 1) * 128], hT[:, kc, c0:c0 + 512],
                                   [rga] + hT_reads(s + 1), last=(kc == KC - 1))
                        k = sg_rot.next()
                        bcol = PV_BIN + (40 if pss == 0 else 48) + j
                        act.op(lambda e: e.activation(sg[k], bank_ap[bi], AF.Sigmoid, bias=pvec[:, bcol:bcol + 1], scale=1.0),
                               reads=[bank_res[bi], r_pvec], writes=[r_sg[k]])
                        bi2 = brot.next()
                        grp = PEGroup(pe, bank_res[bi2])
                        src, rsrc = (u2, r_u2) if pss == 0 else (v, r_v)
                        for kc in range(KC):
                            grp.mm(bank_ap[bi2], wo_[:, kc, jj * 128:(jj + 1) * 128], src[:, kc, s * 512:(s + 1) * 512],
                                   [ro_, rsrc[kc][s]], last=(kc == KC - 1))
                        if pss == 0:
                            dve.op(lambda e: e.scalar_tensor_tensor(m1[:, j, s * 512:(s + 1) * 512], bank_ap[bi2],
                                                                    pvec[:, PV_BAO + j:PV_BAO + j + 1], sg[k],
                                                                    ALU.add, ALU.mult),
                                   reads=[bank_res[bi2], r_pvec, r_sg[k]], writes=[r_m1[j][s]])
                        else:
                            t = tb_rot.next()
                            dve.op(lambda e: e.tensor_tensor(tb_[t], bank_ap[bi2], sg[k], ALU.mult),
                                   reads=[bank_res[bi2], r_sg[k]], writes=[r_tb[t]])
                            dve.op(lambda e: e.tensor_tensor(mg[:, j, s * 512:(s + 1) * 512], m1[:, j, s * 512:(s + 1) * 512],
                                                             tb_[t], ALU.add),
                                   reads=[r_m1[j][s], r_tb[t]], writes=[r_mg[j][s]])
                ws_release(iga)
                ws_release(io_)

        x1 = sb(OFF_X1, [128, NT, D], F32, "x1")
        r_x1 = [newres("x1_%d" % t, OFF_X1 + t * 4096, 4096) for t in range(NT)]
        xt = [sb(OFF_XT[i], [128, D], F32, "xt3") for i in range(3)]
        r_xt = [newres("xt3_%d" % i, OFF_XT[i], 4096) for i in range(3)]
        h2n = [sb(OFF_XN[i], [128, D], BF16, "h2n") for i in range(2)]
        r_h2n = [newres("h2n%d" % i, OFF_XN[i], 2048) for i in range(2)]
        slA = mkslots(4, OFF_TQ)
        slB = mkslots(4, OFF_TQ + 640)
        h2T = sb(OFF_HT, [128, KC, TW], BF16, "h2T")
        _g = new_group(fw, ["h2T%d%s" % (t, ab) for t in range(NT) for ab in "ab"], OFF_HT, OFF_HT + 16896)
        r_h2T = [(_g[2 * t], _g[2 * t + 1]) for t in range(NT)]
        wo0, rwo0, iwo0 = ws_get()
        wo1, rwo1, iwo1 = ws_get()
        wo = [(wo0, rwo0), (wo1, rwo1)]

        def p3_t0(t):
            r0 = row0 + HALO + t * 128
            sp.dma(xt[t % 3], xs[r0:r0 + 128, :], r_xt[t % 3], writes=[r_xt[t % 3]])

        p3bank = {}

        def p3_t1(t):
            bks = []
            for h in range(2):
                bi = brot.next()
                bks.append(bi)
                grp = PEGroup(pe, bank_res[bi])
                grp.mm(bank_ap[bi], ident32, xt[t % 3][:, h * 512:(h + 1) * 512], [r_ident32, r_xt[t % 3]])
                grp.mm(bank_ap[bi], e0, borow[:, h * 512:(h + 1) * 512], [r_const, r_rows])
                for kc in range(KC):
                    grp.mm(bank_ap[bi], mg[:, kc, t * 128:(t + 1) * 128], wo[h][0][:, kc, :],
                           [r_mg[kc][t // 4], wo[h][1]], last=(kc == KC - 1))
            p3bank[t] = bks
            sl = slA[t % 4]
            for h in range(2):
                dve.op(lambda e: e.bn_stats(sl["st"][:, h, :], bank_ap[bks[h]]), reads=[bank_res[bks[h]], sl["res"]], writes=[sl["res"]])
            dve.op(lambda e: e.bn_aggr(sl["mv"], sl["st"]), reads=[sl["res"]], writes=[sl["res"]])

        def p3_t2(t):
            ln_sqrt(128, slA[t % 4], eps2_t)

        def p3_t3(t):
            sl = slA[t % 4]
            bks = p3bank[t]
            ln_recip(128, sl, nmr=False)
            for h in range(2):
                dve.op(lambda e: e.scalar_tensor_tensor(x1[:, t, h * 512:(h + 1) * 512], bank_ap[bks[h]], sl["mv"][:, 0:1],
                                                        bc[:, h * 512:(h + 1) * 512], ALU.subtract, ALU.mult),
                       reads=[bank_res[bks[h]], sl["res"], r_bc], writes=[r_x1[t]])
            dve.op(lambda e: e.scalar_tensor_tensor(x1[:, t, :], x1[:, t, :], sl["sd"][:, 0:1], bc[:, 1024:2048], ALU.mult, ALU.add),
                   reads=[r_x1[t], sl["res"], r_bc], writes=[r_x1[t]])
            ln_stats_a(x1[:, t, :], 128, slB[t % 4], r_x1[t])

        def p3_t4(t):
            ln_sqrt(128, slB[t % 4], eps_t)

        def p3_t5(t):
            ln_recip(128, slB[t % 4])

        def p3_t6(t):
            sl = slB[t % 4]
            act.op(lambda e: e.activation(h2n[t % 2], x1[:, t, :], AF.Identity, bias=sl["nm"], scale=sl["sd"][:, 0:1]),
                   reads=[r_x1[t], sl["res"]], writes=[r_h2n[t % 2]])

        def p3_t7(t):
            transpose_evac(h2n[t % 2], 128, r_h2n[t % 2], h2T, r_h2T[t], HALO + t * 128, SC2, SH2, r_mod2)

        p4 = {}

        def p4_setup():
            p4["fT"] = sb(OFF_FT, [128, FC, TB], BF16, "fT")
            p4["r_fT"] = [[newres("fT_%d_%d" % (fc, s), OFF_FT + (fc * TB + s * 512) * 2, 1024) for s in range(NS)] for fc in range(FC)]
            p4["rr"] = [sb(OFF_XT[0] + i * 1024, [128, 512], BF16, "rr") for i in range(3)]
            p4["r_rr"] = [newres("rr%d" % i, OFF_XT[0] + i * 1024, 1024) for i in range(3)]
            p4["rot"] = Rot(range(3))

        def p4_group(wu, ru, g, jj, s):
            fc = 4 * g + jj
            c0 = HALO + s * 512
            bi = brot.next()
            grp = PEGroup(pe, bank_res[bi])
            for kc in range(KC):
                grp.mm(bank_ap[bi], wu[:, kc, jj * 128:(jj + 1) * 128], h2T[:, kc, c0:c0 + 512],
                       [ru] + [r for pr in r_h2T[s * 4:(s + 1) * 4] for r in pr], last=(kc == KC - 1))
            k = p4["rot"].next()
            rr, r_rr = p4["rr"], p4["r_rr"]
            act.op(lambda e: e.activation(rr[k], bank_ap[bi], AF.Relu, bias=pvec[:, PV_BUP + fc:PV_BUP + fc + 1], scale=1.0),
                   reads=[bank_res[bi], r_pvec], writes=[r_rr[k]])
            dve.op(lambda e: e.tensor_tensor(p4["fT"][:, fc, s * 512:(s + 1) * 512], rr[k], rr[k], ALU.mult),
                   reads=[r_rr[k]], writes=[p4["r_fT"][fc][s]])

        NHEAD = 4
        head = []
        stages = [p3_t0, lambda t: (p3_t1(t), p3_t2(t)), lambda t: (p3_t3(t), p3_t4(t)), lambda t: (p3_t5(t), p3_t6(t)), p3_t7]
        nsteps3 = NT + len(stages) - 1
        for step in range(nsteps3):
            for sidx in reversed(range(len(stages))):
                t = step - sidx
                if 0 <= t < NT:
                    stages[sidx](t)
            if step == NT:
                ws_release(iwo0)
                ws_release(iwo1)
                p4_setup()
            if step >= NT and len(head) < NHEAD:
                g = len(head)
                head.append(ws_get())
                for jj in range(4):
                    p4_group(head[g][0], head[g][1], g, jj, 0)
        while len(head) < NHEAD:
            g = len(head)
            head.append(ws_get())
            for jj in range(4):
                p4_group(head[g][0], head[g][1], g, jj, 0)
        for g in range(NHEAD):
            for jj in range(4):
                p4_group(head[g][0], head[g][1], g, jj, 1)
            ws_release(head[g][2])
        for g in range(NHEAD, 8):
            wu, ru, iu = ws_get()
            for jj in range(4):
                for s in range(NS):
                    p4_group(wu, ru, g, jj, s)
            ws_release(iu)
        fT, r_fT = p4["fT"], p4["r_fT"]

        sl5 = mkslots(8, OFF_TQ)

        def p5_resid(h, t):
            dve.op(lambda e: e.tensor_tensor(x1[:, t, h * 512:(h + 1) * 512], bank_ap[t], x1[:, t, h * 512:(h + 1) * 512], ALU.add),
                   reads=[bank_res[t], r_x1[t]], writes=[r_x1[t]])
            ln_stats_a(x1[:, t, :], 128, sl5[t], r_x1[t], halves=(h,), aggr=(h == 1))

        def p5_e1(t):
            ln_sqrt(128, sl5[t], eps2_t)

        def p5_e2(t):
            ln_recip(128, sl5[t], nmr=False)
            ln_affine_dve(x1[:, t, :], 128, sl5[t], r_x1[t], bc[:, 2048:3072], bc[:, 3072:4096])

        def p5_e3(t):
            r0 = b * TB + t * 128
            ev = sp.dma(y[r0:r0 + 128, :], x1[:, t, :], r_x1[t], reads=[r_x1[t]])
            out_events.append(ev)

        p5_stages = [None, p5_e1, p5_e2, p5_e3]

        def p5_tail_step(t):
            for k_ in (3, 2, 1):
                if 0 <= t - k_ < NT:
                    p5_stages[k_](t - k_)

        def mm_piece(grp, t, h, gq, wd, rd, mark_last):
            if gq == 0:
                grp.mm(bank_ap[t], e0, bdrow[:, h * 512:(h + 1) * 512], [r_const, r_rows])
            for kc in range(KC):
                fc = gq * 8 + kc
                lastmm = (gq == 3 and kc == KC - 1)
                mk = [rd] if (mark_last and kc == KC - 1 and not lastmm) else ()
                grp.mm(bank_ap[t], fT[:, fc, t * 128:(t + 1) * 128], wd[:, kc, :],
                       [r_fT[fc][t // 4], rd], last=lastmm, mark=mk)

        grps = [PEGroup(pe, bank_res[t]) for t in range(NT)]
        for gq in range(4):
            wd, rd, idn = ws_get()
            for t in range(NT):
                mm_piece(grps[t], t, 0, gq, wd, rd, mark_last=(t == NT - 1))
                if gq == 3:
                    p5_resid(0, t)
            ws_release(idn)
        ws_flush()
        grps = [PEGroup(pe, bank_res[t]) for t in range(NT)]
        for gq in range(1):
            wd, rd, idn = ws_get()
            for t in range(NT):
                mm_piece(grps[t], t, 1, gq, wd, rd, mark_last=(t == NT - 1))
            ws_release(idn)
        tl = [ws_get() for _ in range(3)]
        nsteps = []
        if b + 1 < NB:
            ctx, nsteps = p1_setup(b + 1, low_banks=True)
            for _ in range(3):
                nsteps.pop(0)()
        for t in range(NT):
            for q_ in range(3):
                mm_piece(grps[t], t, 1, 1 + q_, tl[q_][0], tl[q_][1], mark_last=False)
            p5_resid(1, t)
            p5_tail_step(t)
            if t >= 1 and nsteps:
                nsteps.pop(0)()
        for q_ in range(3):
            ws_release(tl[q_][2])
        for t in range(NT, NT + 3):
            p5_tail_step(t)

    for ev in out_events:
        sp.wait_ev(ev)
    if needed is None:
        return {k: sorted(v) for k, v in fw.rec.items()}
    return nc


_CACHE = {}


def kernel(x, c, w_ada, b_ada, w_in, b_in, conv_a_w, conv_a_b, ln_a_g, ln_a_b, w_a_out, b_a_out, conv_b_w, w_b_out,
           w_o, b_o, ln1_g, ln1_b, w_up, b_up, w_down, b_down, ln2_g, ln2_b):
    f = lambda a: np.ascontiguousarray(np.asarray(a, dtype=np.float32))
    x2 = f(x).reshape(SEQ, D)

    def pp(vec):
        vec = f(vec).reshape(-1, 128)
        return vec.T

    pv = np.zeros((128, NPV), np.float32)
    pv[:, PV_BIN:PV_BIN + 56] = pp(b_in)
    pv[:, PV_CAB:PV_CAB + 8] = pp(conv_a_b)
    pv[:, PV_LAG:PV_LAG + 8] = pp(ln_a_g)
    pv[:, PV_LAB:PV_LAB + 8] = pp(ln_a_b)
    pv[:, PV_BAO:PV_BAO + 8] = pp(b_a_out)
    pv[:, PV_BUP:PV_BUP + 32] = pp(b_up)
    caw = f(conv_a_w).reshape(31, 8, 128)
    pv[:, PV_CAW:PV_CAW + 248] = caw.transpose(2, 1, 0).reshape(128, 248)
    cbw = f(conv_b_w).reshape(3, 8, 128)
    pv[:, PV_CBW:PV_CBW + 24] = cbw.transpose(2, 1, 0).reshape(128, 24)
    pv[:, PV_C:PV_C + 8] = pp(c)
    rows = np.concatenate([f(b_ada).reshape(-1), f(b_o).reshape(-1), f(b_down).reshape(-1)])[None, :]
    bcv = np.concatenate([f(ln1_g).reshape(-1), f(ln1_b).reshape(-1), f(ln2_g).reshape(-1), f(ln2_b).reshape(-1)])
    bcm = np.ascontiguousarray(np.broadcast_to(bcv[None, :], (128, 4096)))
    wts = dict(w_ada=f(w_ada).reshape(D, 6 * D), w_in=f(w_in).reshape(D, 7 * D), w_a_out=f(w_a_out).reshape(D, D),
               w_b_out=f(w_b_out).reshape(D, D), w_o=f(w_o).reshape(D, D), w_up=f(w_up).reshape(D, DFF),
               w_down=f(w_down).reshape(DFF, D))
    in_maps = []
    for k in range(NCORES):
        xsk = np.zeros((TPC + HALO, D), np.float32)
        if k == 0:
            xsk[HALO:] = x2[0:TPC]
        else:
            xsk[:] = x2[k * TPC - HALO:(k + 1) * TPC]
        pvk = pv.copy()
        pvk[:, PV_MASK] = 0.0 if k == 0 else 1.0
        pvk[:, PV_MASK + 1] = 1.0
        m = dict(xs=xsk, pvec=pvk, rows=rows, bc=bcm)
        m.update(wts)
        in_maps.append(m)
    if "nc" not in _CACHE:
        rec = build_program(None)
        _CACHE["nc"] = build_program(rec)
    res = run_bass_kernel_spmd(_CACHE["nc"], in_maps, core_ids=list(range(NCORES)))
    out = np.concatenate([np.asarray(r["y"]) for r in res.results], axis=0)
    return out.reshape(1, SEQ, D).astype(np.float32)
```
